# Optimizing a Trainium2 kernel written in Bass

```python
import math
import jax
import jax.numpy as jnp
from jax import lax
import numpy as np

D_MODEL = 1024
BATCH = 16
SEQ = 2048
DEPTH = 4

GRID_W = 64
CTX_LEN = 256

SSD_HEADS = 16
SSD_HEADDIM = 64
SSD_D_INNER = SSD_HEADS * SSD_HEADDIM
SSD_GROUPS = 4
SSD_STATE = 128
SSD_CONV = 5
SSD_CHUNK = 128
SSD_XBC = SSD_D_INNER + 2 * SSD_GROUPS * SSD_STATE

NA_HEADS = 16
NA_HEAD_DIM = 64
NA_WIDTH = NA_HEADS * NA_HEAD_DIM
NA_WIN_ROWS = 8
NA_WIN_COLS = 16

POOL_WINDOWS = (2, 4, 8, 16)
POOL_GROUP = 192
POOL_WIDTH = 4 * POOL_GROUP

FNET_HEADS = 4
FNET_HEAD_DIM = 192
FNET_WIDTH = FNET_HEADS * FNET_HEAD_DIM

N_BRANCH = 4
IN_WIDTHS = (SSD_D_INNER, SSD_XBC, 2 * SSD_HEADS, 3 * NA_WIDTH, POOL_WIDTH, FNET_WIDTH, N_BRANCH * D_MODEL)
IN_WIDTH = SSD_D_INNER + SSD_XBC + 2 * SSD_HEADS + 3 * NA_WIDTH + POOL_WIDTH + FNET_WIDTH + N_BRANCH * D_MODEL

D_FF = 4 * D_MODEL
ROPE_THETA = 10000.0
DEEPNORM_ALPHA = (2 * DEPTH) ** 0.25
DEEPNORM_BETA = (8 * DEPTH) ** -0.25
LN_EPS = 1e-6
RMS_EPS = 1e-5

kernel_name = 'hybrid_ssd_natten_pool_fourier_dit'

F32 = jnp.float32


def layer_norm(x, gain=None, bias=None):
    xf = x.astype(F32)
    mu = jnp.mean(xf, -1, keepdims=True)
    var = jnp.mean(jnp.square(xf - mu), -1, keepdims=True)
    y = (xf - mu) * lax.rsqrt(var + LN_EPS)
    if gain is not None:
        y = y * gain.astype(F32) + bias.astype(F32)
    return y.astype(x.dtype)


def modulate(h, shift, scale):
    return h * (1 + scale) + shift


def split_in_proj(p):
    idx = np.cumsum(np.array(IN_WIDTHS))[:-1].tolist()
    return jnp.split(p, idx, axis=-1)


def axial_rope(n, dim):
    pos = jnp.arange(n)
    quarter = dim // 4
    inv_freq = ROPE_THETA ** (-jnp.arange(quarter, dtype=F32) / quarter)
    row = (pos // GRID_W).astype(F32)[:, None] * inv_freq
    col = (pos % GRID_W).astype(F32)[:, None] * inv_freq
    ang = jnp.concatenate([row, col], -1)
    return jnp.cos(ang), jnp.sin(ang)


def apply_rope(t, cos, sin):
    half = t.shape[-1] // 2
    tf = t.astype(F32)
    t1, t2 = tf[..., :half], tf[..., half:]
    c, s = cos[None, :, None, :], sin[None, :, None, :]
    return jnp.concatenate([t1 * c - t2 * s, t1 * s + t2 * c], -1).astype(t.dtype)


def depthwise_conv_centred(x, w, b):
    k = w.shape[0]
    y = lax.conv_general_dilated(x, w.astype(x.dtype)[:, None, :], window_strides=(1,),
                                 padding=((k // 2, k // 2),), dimension_numbers=('NWC', 'WIO', 'NWC'),
                                 feature_group_count=x.shape[-1])
    return y + b


def grouped_rms_norm(y, w):
    bsz, n, d = y.shape
    yf = y.astype(F32).reshape(bsz, n, SSD_GROUPS, d // SSD_GROUPS)
    yf = yf * lax.rsqrt(jnp.mean(jnp.square(yf), -1, keepdims=True) + RMS_EPS)
    return (yf.reshape(bsz, n, d) * w.astype(F32)).astype(y.dtype)


def ssd_chunked(xs, dt, a, bm, cm, init_state):
    bsz, n, nh, hp = xs.shape
    ng, ns = bm.shape[2], bm.shape[3]
    nr = nh // ng
    t = SSD_CHUNK
    nc = n // t
    xdt = (xs.astype(F32) * dt[..., None]).reshape(bsz, nc, t, ng, nr, hp)
    a_cum = jnp.cumsum((dt * a.astype(F32)).reshape(bsz, nc, t, ng, nr), axis=2)
    bmc = bm.astype(F32).reshape(bsz, nc, t, ng, ns)
    cmc = cm.astype(F32).reshape(bsz, nc, t, ng, ns)
    lower = jnp.tril(jnp.ones((t, t), dtype=bool))[:, :, None, None]
    seg = a_cum[:, :, :, None] - a_cum[:, :, None, :]
    decay = jnp.exp(jnp.where(lower, seg, -jnp.inf))
    cb = jnp.einsum('bclgn,bcsgn->bclsg', cmc, bmc)
    y_diag = jnp.einsum('bclsg,bclsgr,bcsgrp->bclgrp', cb, decay, xdt)
    decay_to_end = jnp.exp(a_cum[:, :, -1:] - a_cum)
    chunk_states = jnp.einsum('bclgn,bclgr,bclgrp->bcgrpn', bmc, decay_to_end, xdt)
    chunk_decay = jnp.exp(a_cum[:, :, -1])

    def step(state, inp):
        s_c, d_c = inp
        return state * d_c[..., None, None] + s_c, state

    final, states_in = lax.scan(step, init_state,
                                (jnp.moveaxis(chunk_states, 1, 0), jnp.moveaxis(chunk_decay, 1, 0)))
    states_in = jnp.moveaxis(states_in, 0, 1)
    y_off = jnp.einsum('bclgn,bcgrpn,bclgr->bclgrp', cmc, states_in, jnp.exp(a_cum))
    return (y_diag + y_off).reshape(bsz, n, nh, hp), final


def ssd_bidirectional(xs, dt, bm, cm, a, init_fwd, init_bwd):
    rev = lambda t: jnp.flip(t, axis=1)
    y_f, s_f = ssd_chunked(xs, dt[:, :, 0], a[0], bm, cm, init_fwd)
    y_b, s_b = ssd_chunked(rev(xs), rev(dt[:, :, 1]), a[1], rev(bm), rev(cm), init_bwd)
    return (y_f + rev(y_b)).astype(xs.dtype), s_f, s_b


def ssd_prepare(xbc, dt_raw, lp, use_rope):
    xbc = jax.nn.silu(depthwise_conv_centred(xbc, lp['ssd_conv_w'], lp['ssd_conv_b']))
    xs, bm, cm = jnp.split(xbc, [SSD_D_INNER, SSD_D_INNER + SSD_GROUPS * SSD_STATE], axis=-1)
    bsz, n = xs.shape[0], xs.shape[1]
    xs = xs.reshape(bsz, n, SSD_HEADS, SSD_HEADDIM)
    bm = bm.reshape(bsz, n, SSD_GROUPS, SSD_STATE)
    cm = cm.reshape(bsz, n, SSD_GROUPS, SSD_STATE)
    if use_rope:
        cos, sin = axial_rope(n, SSD_STATE)
        bm = apply_rope(bm, cos, sin)
        cm = apply_rope(cm, cos, sin)
    dt = jax.nn.softplus(dt_raw.astype(F32).reshape(bsz, n, 2, SSD_HEADS) + lp['ssd_dt_bias'].astype(F32))
    return xs, dt, bm, cm


def ssd_output(y, xs, z, lp):
    bsz, n = xs.shape[0], xs.shape[1]
    y = (y + lp['ssd_d'][:, None] * xs).reshape(bsz, n, SSD_D_INNER)
    y = grouped_rms_norm(y * jax.nn.silu(z), lp['ssd_norm_w'])
    return y @ lp['ssd_out_w']


def split_qkv(qkv):
    bsz, n = qkv.shape[0], qkv.shape[1]
    qkv = qkv.reshape(bsz, n, 3, NA_HEADS, NA_HEAD_DIM)
    return qkv[:, :, 0] * (NA_HEAD_DIM ** -0.5), qkv[:, :, 1], qkv[:, :, 2]


def neighbourhood_attention(q, k, v, k_ctx, v_ctx, rpb):
    bsz, n, nh, hd = q.shape
    rows = n // GRID_W
    kr = min(NA_WIN_ROWS, rows)
    kc = NA_WIN_COLS
    n_loc = kr * kc
    q_g = q.reshape(bsz, rows, GRID_W, nh, hd)
    k_g = k.reshape(bsz, rows, GRID_W, nh, hd)
    v_g = v.reshape(bsz, rows, GRID_W, nh, hd)
    row_start = jnp.clip(jnp.arange(rows) - kr // 2, 0, rows - kr)
    cols = jnp.arange(GRID_W)
    col_idx = jnp.clip(cols - kc // 2, 0, GRID_W - kc)[:, None] + jnp.arange(kc)[None, :]
    dc_idx = (col_idx - cols[:, None] + NA_WIN_COLS - 1)[None]

    def one_row(r):
        rs = row_start[r]
        q_r = lax.dynamic_index_in_dim(q_g, r, axis=1, keepdims=False)
        k_win = lax.dynamic_slice_in_dim(k_g, rs, kr, axis=1)[:, :, col_idx]
        v_win = lax.dynamic_slice_in_dim(v_g, rs, kr, axis=1)[:, :, col_idx]
        dr_idx = (rs + jnp.arange(kr) - r + NA_WIN_ROWS - 1)[:, None, None]
        bias = rpb[:, dr_idx, dc_idx].transpose(0, 2, 1, 3).astype(F32)
        s_loc = jnp.einsum('bjhd,bajchd->bhjac', q_r, k_win).astype(F32) + bias
        s_ctx = jnp.einsum('bjhd,bmhd->bhjm', q_r, k_ctx).astype(F32)
        p = jax.nn.softmax(jnp.concatenate([s_loc.reshape(bsz, nh, GRID_W, n_loc), s_ctx], -1), -1)
        p = p.astype(v.dtype)
        o = jnp.einsum('bhjac,bajchd->bjhd', p[..., :n_loc].reshape(bsz, nh, GRID_W, kr, kc), v_win)
        return o + jnp.einsum('bhjm,bmhd->bjhd', p[..., n_loc:], v_ctx)

    out = lax.map(one_row, jnp.arange(rows))
    return jnp.moveaxis(out, 0, 1).reshape(bsz, n, nh * hd)


def context_attention(q, k, v):
    s = jnp.einsum('bqhd,bkhd->bhqk', q, k).astype(F32)
    p = jax.nn.softmax(s, -1).astype(v.dtype)
    o = jnp.einsum('bhqk,bkhd->bqhd', p, v)
    return o.reshape(q.shape[0], q.shape[1], NA_WIDTH)


def pool_branch(u, lp):
    bsz, n, _ = u.shape
    uf = u.astype(F32)
    csum = jnp.pad(jnp.cumsum(uf, axis=1), ((0, 0), (1, 0), (0, 0)))
    pos = jnp.arange(n)
    parts = []
    for gi, win in enumerate(POOL_WINDOWS):
        sl = slice(gi * POOL_GROUP, (gi + 1) * POOL_GROUP)
        lo = jnp.clip(pos - win // 2, 0, n)
        hi = jnp.clip(pos + win - win // 2, 0, n)
        cnt = (hi - lo).astype(F32)[None, :, None]
        parts.append((csum[:, hi, sl] - csum[:, lo, sl]) / cnt - uf[:, :, sl])
    pooled = jnp.concatenate(parts, -1).astype(u.dtype).reshape(bsz, n, len(POOL_WINDOWS), POOL_GROUP)
    mapped = jnp.einsum('bngc,gcd->bngd', pooled, lp['pool_w']).reshape(bsz, n, POOL_WIDTH)
    return (mapped * lp['pool_scale']) @ lp['pool_out_w']


def fourier_branch(u, lp):
    bsz, n, _ = u.shape
    uf = u.astype(F32).reshape(bsz, n, FNET_HEADS, FNET_HEAD_DIM)
    mixed = jnp.fft.fft2(uf, axes=(1, 3), norm='ortho').real
    return mixed.reshape(bsz, n, FNET_WIDTH).astype(u.dtype) @ lp['fnet_out_w']


def merge_and_project(y_ssd, y_na, y_pool, y_fnet, gate_logits, w_out):
    g = jax.nn.sigmoid(gate_logits.astype(F32)).astype(y_ssd.dtype)
    g_ssd, g_na, g_pool, g_fnet = jnp.split(g, N_BRANCH, axis=-1)
    return (g_ssd * y_ssd + g_na * y_na + g_pool * y_pool + g_fnet * y_fnet) @ w_out


def token_mixing(h, hc, lp, need_ctx):
    z_x, xbc_x, dt_x, qkv_x, pool_x, fnet_x, gate_x = split_in_proj(h @ lp['in_w'] + lp['in_b'])
    z_c, xbc_c, dt_c, qkv_c, pool_c, fnet_c, gate_c = split_in_proj(hc @ lp['in_w'] + lp['in_b'])

    a = -jnp.exp(lp['ssd_a_log'].astype(F32))
    xs_c, dtc, bm_c, cm_c = ssd_prepare(xbc_c, dt_c, lp, False)
    xs_x, dtx, bm_x, cm_x = ssd_prepare(xbc_x, dt_x, lp, True)
    zero = jnp.zeros((hc.shape[0], SSD_GROUPS, SSD_HEADS // SSD_GROUPS, SSD_HEADDIM, SSD_STATE), F32)
    y_c, s_fwd, s_bwd = ssd_bidirectional(xs_c, dtc, bm_c, cm_c, a, zero, zero)
    y_x, _, _ = ssd_bidirectional(xs_x, dtx, bm_x, cm_x, a, s_fwd, s_bwd)
    ssd_lat = ssd_output(y_x, xs_x, z_x, lp)

    q_x, k_x, v_x = split_qkv(qkv_x)
    q_c, k_c, v_c = split_qkv(qkv_c)
    na_lat = neighbourhood_attention(q_x, k_x, v_x, k_c, v_c, lp['na_rpb']) @ lp['na_out_w']

    mix_x = merge_and_project(ssd_lat, na_lat, pool_branch(pool_x, lp), fourier_branch(fnet_x, lp),
                              gate_x, lp['mix_out_w'])
    mix_c = None
    if need_ctx:
        ssd_ctx = ssd_output(y_c, xs_c, z_c, lp)
        na_ctx = context_attention(q_c, k_c, v_c) @ lp['na_out_w']
        mix_c = merge_and_project(ssd_ctx, na_ctx, pool_branch(pool_c, lp), fourier_branch(fnet_c, lp),
                                  gate_c, lp['mix_out_w'])
    return mix_x, mix_c


def squared_relu_mlp(h, lp):
    a = jax.nn.relu(h @ lp['mlp_up_w'] + lp['mlp_up_b'])
    return jnp.square(a) @ lp['mlp_down_w'] + lp['mlp_down_b']


def setup_inputs(seed: int = 0) -> dict:
    key = jax.random.key(seed)
    ks = list(jax.random.split(key, 32))
    L = DEPTH

    def nrm(i, shape, scale):
        return jax.random.normal(ks[i], shape, F32) * scale

    dt0 = jnp.exp(jax.random.uniform(ks[10], (L, 2, SSD_HEADS), F32, math.log(1e-3), math.log(1e-1)))
    return {
        'x': nrm(0, (BATCH, SEQ, D_MODEL), 1.0),
        'c': nrm(1, (BATCH, D_MODEL), 1.0),
        'ctx': nrm(2, (BATCH, CTX_LEN, D_MODEL), 1.0),
        'c_ctx': nrm(3, (D_MODEL,), 1.0),
        'ada_w': nrm(4, (L, D_MODEL, 6 * D_MODEL), 0.5 * D_MODEL ** -0.5),
        'ada_b': nrm(5, (L, 6 * D_MODEL), 0.02),
        'in_w': nrm(6, (L, D_MODEL, IN_WIDTH), D_MODEL ** -0.5),
        'in_b': nrm(7, (L, IN_WIDTH), 0.02),
        'ssd_conv_w': nrm(8, (L, SSD_CONV, SSD_XBC), SSD_CONV ** -0.5),
        'ssd_conv_b': nrm(9, (L, SSD_XBC), 0.02),
        'ssd_dt_bias': dt0 + jnp.log(-jnp.expm1(-dt0)),
        'ssd_a_log': jnp.log(jax.random.uniform(ks[11], (L, 2, SSD_HEADS), F32, 1.0, 16.0)),
        'ssd_d': 1.0 + nrm(12, (L, SSD_HEADS), 0.1),
        'ssd_norm_w': 1.0 + nrm(13, (L, SSD_D_INNER), 0.02),
        'ssd_out_w': nrm(14, (L, SSD_D_INNER, D_MODEL), SSD_D_INNER ** -0.5),
        'na_rpb': nrm(15, (L, NA_HEADS, 2 * NA_WIN_ROWS - 1, 2 * NA_WIN_COLS - 1), 0.02),
        'na_out_w': nrm(16, (L, NA_WIDTH, D_MODEL), NA_WIDTH ** -0.5),
        'pool_w': nrm(17, (L, len(POOL_WINDOWS), POOL_GROUP, POOL_GROUP), POOL_GROUP ** -0.5),
        'pool_scale': 1.0 + nrm(18, (L, POOL_WIDTH), 0.02),
        'pool_out_w': nrm(19, (L, POOL_WIDTH, D_MODEL), POOL_WIDTH ** -0.5),
        'fnet_out_w': nrm(20, (L, FNET_WIDTH, D_MODEL), FNET_WIDTH ** -0.5),
        'mix_out_w': nrm(21, (L, D_MODEL, D_MODEL), DEEPNORM_BETA * D_MODEL ** -0.5),
        'ln1_g': 1.0 + nrm(22, (L, D_MODEL), 0.02),
        'ln1_b': nrm(23, (L, D_MODEL), 0.02),
        'mlp_up_w': nrm(24, (L, D_MODEL, D_FF), D_MODEL ** -0.5),
        'mlp_up_b': nrm(25, (L, D_FF), 0.02),
        'mlp_down_w': nrm(26, (L, D_FF, D_MODEL), DEEPNORM_BETA * D_FF ** -0.5),
        'mlp_down_b': nrm(27, (L, D_MODEL), 0.02),
        'ln2_g': 1.0 + nrm(28, (L, D_MODEL), 0.02),
        'ln2_b': nrm(29, (L, D_MODEL), 0.02),
    }


def reference(x, c, ctx, c_ctx, ada_w, ada_b, in_w, in_b, ssd_conv_w, ssd_conv_b, ssd_dt_bias, ssd_a_log,
              ssd_d, ssd_norm_w, ssd_out_w, na_rpb, na_out_w, pool_w, pool_scale, pool_out_w, fnet_out_w,
              mix_out_w, ln1_g, ln1_b, mlp_up_w, mlp_up_b, mlp_down_w, mlp_down_b, ln2_g, ln2_b):
    alpha = DEEPNORM_ALPHA
    for i in range(DEPTH):
        lp = {
            'in_w': in_w[i], 'in_b': in_b[i], 'ssd_conv_w': ssd_conv_w[i], 'ssd_conv_b': ssd_conv_b[i],
            'ssd_dt_bias': ssd_dt_bias[i], 'ssd_a_log': ssd_a_log[i], 'ssd_d': ssd_d[i],
            'ssd_norm_w': ssd_norm_w[i], 'ssd_out_w': ssd_out_w[i], 'na_rpb': na_rpb[i],
            'na_out_w': na_out_w[i], 'pool_w': pool_w[i], 'pool_scale': pool_scale[i],
            'pool_out_w': pool_out_w[i], 'fnet_out_w': fnet_out_w[i], 'mix_out_w': mix_out_w[i],
            'mlp_up_w': mlp_up_w[i], 'mlp_up_b': mlp_up_b[i], 'mlp_down_w': mlp_down_w[i],
            'mlp_down_b': mlp_down_b[i],
        }
        need_ctx = i < DEPTH - 1
        mod_x = (jax.nn.silu(c) @ ada_w[i] + ada_b[i])[:, None, :]
        mod_c = (jax.nn.silu(c_ctx) @ ada_w[i] + ada_b[i])[None, None, :]
        sh1, sc1, g1, sh2, sc2, g2 = jnp.split(mod_x, 6, axis=-1)
        csh1, csc1, cg1, csh2, csc2, cg2 = jnp.split(mod_c, 6, axis=-1)

        h = modulate(layer_norm(x), sh1, sc1)
        hc = modulate(layer_norm(ctx), csh1, csc1)
        mix_x, mix_c = token_mixing(h, hc, lp, need_ctx)

        x = layer_norm(alpha * x + g1 * mix_x, ln1_g[i], ln1_b[i])
        x = layer_norm(alpha * x + g2 * squared_relu_mlp(modulate(layer_norm(x), sh2, sc2), lp), ln2_g[i], ln2_b[i])
        if need_ctx:
            ctx = layer_norm(alpha * ctx + cg1 * mix_c, ln1_g[i], ln1_b[i])
            ctx = layer_norm(alpha * ctx + cg2 * squared_relu_mlp(modulate(layer_norm(ctx), csh2, csc2), lp),
                             ln2_g[i], ln2_b[i])
    return x
```

```python
import math
import numpy as np
import ml_dtypes
import concourse.bass as bass
import concourse.mybir as mybir
from concourse.bass_utils import run_bass_kernel_spmd
from contextlib import ExitStack

F32 = mybir.dt.float32
BF16 = mybir.dt.bfloat16
AF = mybir.ActivationFunctionType
ALU = mybir.AluOpType

D = 1024
NB = 2
SEQ = 2048
CTX = 256
TS = SEQ + CTX
TT = NB * TS
NTILE = TT // 128
NCH = TT // 512
DEPTH = 4
INW = 11808
NEG = -30000.0
ALPHA = (2 * DEPTH) ** 0.25

C_Z, C_XBC, C_DT, C_Q, C_K, C_V, C_POOL, C_FN, C_G = 0, 1024, 3072, 3104, 4128, 5152, 6176, 6944, 7712


def fm_tiles():
    t = []
    for i in range(16):
        t.append(("xbc", C_XBC + 128 * i, 128, 128 * i))
    for i in range(8):
        t.append(("q", C_Q + 128 * i, 128, 128 * i))
    for i in range(8):
        t.append(("k", C_K + 128 * i, 128, 128 * i))
    for g in range(4):
        t.append(("pool", C_POOL + 192 * g, 128, 192 * g))
        t.append(("pool", C_POOL + 192 * g + 128, 64, 192 * g + 128))
    for i in range(32):
        t.append(("gate", C_G + 128 * i, 128, 128 * i))
    return t


FM_TILES = fm_tiles()


def seq_tiles(b):
    return [2 * b, 2 * b + 1] + [4 + 16 * b + i for i in range(16)]


def tile_modrow(t):
    if t < 4:
        return 2
    return (t - 4) // 16


class T:
    __slots__ = ("lw", "rs")

    def __init__(self):
        self.lw = None
        self.rs = []


class Prog:
    ENG = ("pe", "act", "dve", "pool", "sp")

    def __init__(self, n_dma_sems=40):
        self.nc = bass.Bass("TRN2", target_bir_lowering=False)
        self.q = {e: [] for e in self.ENG}
        self.es = ExitStack()
        self.n_dma_sems = n_dma_sems
        self.dma_cnt = [0] * n_dma_sems
        self.dma_last = [None] * n_dma_sems
        self.dma_next = 0
        self.dma_rr = {}
        self.epoch = 0
        self.nops = 0

    def sbuf(self, name, shape, dt):
        return self.es.enter_context(self.nc.sbuf_tensor(name, list(shape), dt))

    def psum(self, name, shape, dt):
        return self.es.enter_context(self.nc.psum_tensor(name, list(shape), dt))

    def dram(self, name, shape, dt, kind="Internal"):
        return self.nc.dram_tensor(name, list(shape), dt, kind=kind)

    def _deps(self, eng, reads, writes, is_dma):
        idx = len(self.q[eng])
        me = (eng, idx)
        deps = set()
        for t in reads:
            if t.lw is not None:
                deps.add((t.lw, 0))
            t.rs.append(me)
        for t in writes:
            if t.lw is not None:
                deps.add((t.lw, 1))
            for r in t.rs:
                if r != me:
                    deps.add((r, 2))
            t.lw = me
            t.rs = []
        out = set()
        for (p, kind) in deps:
            if p == me:
                continue
            prec = self.q[p[0]][p[1]]
            if prec["dma"] is not None or is_dma:
                out.add(p)
            elif p[0] == eng:
                if kind == 0 and eng != "pe":
                    out.add(p)
            else:
                out.add(p)
        return out

    def _slot(self, eng):
        if eng == "sp":
            lo, hi = 0, self.n_dma_sems - 12
        else:
            lo, hi = self.n_dma_sems - 12, self.n_dma_sems
        c = self.dma_rr.get(eng, 0)
        self.dma_rr[eng] = c + 1
        return lo + c % (hi - lo)

    def op(self, eng, fn, reads=(), writes=()):
        deps = self._deps(eng, reads, writes, False)
        self.q[eng].append({"fn": fn, "deps": deps, "dma": None, "sig": False, "ep": self.epoch})
        self.nops += 1

    def dma(self, eng, out, in_, reads=(), writes=()):
        deps = self._deps(eng, reads, writes, True)
        slot = self._slot(eng)
        prev = self.dma_cnt[slot]
        self.dma_cnt[slot] += 16
        self.dma_last[slot] = (eng, len(self.q[eng]))
        self.q[eng].append({"fn": (out, in_), "deps": deps, "dma": (slot, prev, prev + 16), "sig": False, "ep": self.epoch})
        self.nops += 1

    def dma_fn(self, eng, fn, reads=(), writes=()):
        deps = self._deps(eng, reads, writes, True)
        slot = self._slot(eng)
        prev = self.dma_cnt[slot]
        self.dma_cnt[slot] += 16
        self.dma_last[slot] = (eng, len(self.q[eng]))
        self.q[eng].append({"fn": fn, "deps": deps, "dma": (slot, prev, prev + 16), "sig": False, "ep": self.epoch})
        self.nops += 1

    def barrier(self, new_epoch=False):
        last = {e: len(self.q[e]) - 1 for e in self.ENG}
        for e in self.ENG:
            deps = set()
            for e2 in self.ENG:
                if e2 != e and last[e2] >= 0:
                    deps.add((e2, last[e2]))
            for s in range(self.n_dma_sems):
                if self.dma_last[s] is not None:
                    deps.add(self.dma_last[s])
            self.q[e].append({"fn": None, "deps": deps, "dma": None, "sig": False, "ep": self.epoch})
        if new_epoch:
            self.epoch += 1

    def emit(self):
        nc = self.nc
        q = self.q
        for e in self.ENG:
            for rec in q[e]:
                for (pe_, pi) in rec["deps"]:
                    prec = q[pe_][pi]
                    if prec["fn"] is None:
                        j = pi
                        while j >= 0 and q[pe_][j]["fn"] is None:
                            j -= 1
                        if j >= 0:
                            q[pe_][j]["sig"] = True
                    else:
                        prec["sig"] = True
        nep = self.epoch + 1
        for e in self.ENG:
            c = [0] * nep
            for rec in q[e]:
                if rec["fn"] is not None and rec["dma"] is None and rec["sig"]:
                    c[rec["ep"]] += 1
                    rec["cnt"] = c[rec["ep"]]
        es = self.es
        esem = {(e, k): es.enter_context(nc.semaphore("s_%s_%d" % (e, k))) for e in self.ENG for k in range(nep)}
        dsem = [es.enter_context(nc.semaphore("d%d" % i)) for i in range(self.n_dma_sems)]
        block = es.enter_context(nc.Block())

        def resolve(pe_, pi):
            prec = q[pe_][pi]
            if prec["fn"] is None:
                j = pi
                while j >= 0 and q[pe_][j]["fn"] is None:
                    j -= 1
                if j < 0:
                    return None
                prec = q[pe_][j]
            return prec

        def run(ename, eobj):
            waited = {}
            for rec in q[ename]:
                need = {}
                for (pe_, pi) in rec["deps"]:
                    prec = resolve(pe_, pi)
                    if prec is None:
                        continue
                    if prec["dma"] is not None:
                        key = ("d", prec["dma"][0])
                        val = prec["dma"][2]
                    else:
                        key = ("e", pe_, prec["ep"])
                        val = prec["cnt"]
                    if need.get(key, 0) < val:
                        need[key] = val
                if rec["dma"] is not None:
                    slot, prev, tgt = rec["dma"]
                    if prev > 0:
                        key = ("d", slot)
                        if need.get(key, 0) < prev:
                            need[key] = prev
                for key, val in need.items():
                    if waited.get(key, 0) >= val:
                        continue
                    waited[key] = val
                    sem = dsem[key[1]] if key[0] == "d" else esem[(key[1], key[2])]
                    eobj.wait_ge(sem, val)
                if rec["fn"] is None:
                    continue
                if rec["dma"] is not None:
                    if callable(rec["fn"]):
                        rec["fn"](eobj).then_inc(dsem[rec["dma"][0]], 16)
                    else:
                        out, in_ = rec["fn"]
                        eobj.dma_start(out=out, in_=in_).then_inc(dsem[rec["dma"][0]], 16)
                else:
                    ins = rec["fn"](eobj)
                    if rec["sig"]:
                        ins.then_inc(esem[(ename, rec["ep"])], 1)
            if ename == "sp":
                for i in range(self.n_dma_sems):
                    if self.dma_cnt[i] > 0 and waited.get(("d", i), 0) < self.dma_cnt[i]:
                        eobj.wait_ge(dsem[i], self.dma_cnt[i])

        @block.tensor
        def _(e):
            run("pe", e)

        @block.scalar
        def _(e):
            run("act", e)

        @block.vector
        def _(e):
            run("dve", e)

        @block.gpsimd
        def _(e):
            run("pool", e)

        @block.sync
        def _(e):
            run("sp", e)

    def close(self):
        self.es.close()


class Arena:
    def __init__(self, P, nbytes):
        self.t = P.sbuf("arena", [128, nbytes // 4], F32)
        self.nbytes = nbytes
        self.off = 0

    def reset(self):
        self.off = 0

    def alloc(self, shape, dt):
        esz = 2 if dt == BF16 else 4
        n = 1
        for s in shape[1:]:
            n *= s
        nb = (n * esz + 31) // 32 * 32
        assert self.off + nb <= self.nbytes, ("arena overflow", self.off, nb, self.nbytes)
        a = self.t[:, self.off // 4:(self.off + nb) // 4]
        self.off += nb
        if dt == BF16:
            a = a.bitcast(BF16)
        a = a[:, 0:n]
        if len(shape) == 3:
            a = a.rearrange("p (a b) -> p a b", a=shape[1])
        elif len(shape) == 4:
            a = a.rearrange("p (a b c) -> p a b c", a=shape[1], b=shape[2])
        elif len(shape) == 5:
            a = a.rearrange("p (a b c d) -> p a b c d", a=shape[1], b=shape[2], c=shape[3])
        if shape[0] < 128:
            a = a[0:shape[0]]
        return a


def host_consts():
    c = {}
    bf = ml_dtypes.bfloat16
    c["ident_bf"] = np.eye(128, dtype=np.float32).astype(bf)
    c["ident_f"] = np.eye(128, dtype=np.float32)
    s = np.arange(128)[:, None]
    l = np.arange(128)[None, :]
    c["tri"] = np.stack([(s <= l), (s >= l)]).astype(np.float32)
    c["maskneg"] = np.stack([np.where(s <= l, 0.0, NEG), np.where(s >= l, 0.0, NEG)]).astype(np.float32).astype(bf)
    oh = np.zeros((16, 16, 128), np.float32)
    for h in range(16):
        oh[h, h, :] = 1.0
    c["onehot"] = oh.transpose(1, 0, 2).copy()
    rm = np.zeros((128, 128), np.float32)
    for i in range(64):
        rm[i + 64, i] = -1.0
        rm[i, i + 64] = 1.0
    c["rotm"] = rm
    pos = np.arange(SEQ)
    inv = 10000.0 ** (-np.arange(32, dtype=np.float64) / 32)
    ang = np.concatenate([(pos // 64)[:, None] * inv, (pos % 64)[:, None] * inv], -1)
    c["ropecos"] = np.concatenate([np.cos(ang), np.cos(ang)], -1).T.astype(np.float32).copy()
    c["ropesin"] = np.concatenate([np.sin(ang), np.sin(ang)], -1).T.astype(np.float32).copy()
    PW = 8 + CTX + 16 + SEQ + 8
    ic = np.zeros((4, PW), np.float32)
    for gi, win in enumerate((2, 4, 8, 16)):
        for (n, off) in ((CTX, 8), (SEQ, 8 + CTX + 16)):
            p = np.arange(n)
            lo = np.clip(p - win // 2, 0, n)
            hi = np.clip(p + win - win // 2, 0, n)
            ic[gi, off:off + n] = 1.0 / (hi - lo)
    c["invcnt"] = ic
    for n, nm in ((SEQ, "2048"), (CTX, "256")):
        k = np.arange(n, dtype=np.float64)
        a = 2 * np.pi * np.outer(k, k) / n
        c["dftc" + nm] = (np.cos(a) / math.sqrt(n)).astype(np.float32).astype(bf)
        c["dfts" + nm] = (np.sin(a) / math.sqrt(n)).astype(np.float32).astype(bf)
    k = np.arange(192, dtype=np.float64)
    a = 2 * np.pi * np.outer(k, k) / 192
    c["dftcd"] = (np.cos(a) / math.sqrt(192)).astype(np.float32).astype(bf)
    c["dftsdn"] = (-np.sin(a) / math.sqrt(192)).astype(np.float32).astype(bf)
    return c


CONST_SPECS = {
    "ident_bf": ([128, 128], BF16), "ident_f": ([128, 128], F32), "tri": ([2, 128, 128], F32),
    "maskneg": ([2, 128, 128], BF16), "onehot": ([16, 16, 128], F32), "rotm": ([128, 128], F32),
    "ropecos": ([128, 2048], F32), "ropesin": ([128, 2048], F32), "invcnt": ([4, 8 + CTX + 16 + SEQ + 8], F32),
    "dftc2048": ([2048, 2048], BF16), "dfts2048": ([2048, 2048], BF16), "dftc256": ([256, 256], BF16),
    "dfts256": ([256, 256], BF16), "dftcd": ([192, 192], BF16), "dftsdn": ([192, 192], BF16),
}

W_SPECS = {
    "ada_w": [DEPTH, D, 6 * D], "ada_b": [DEPTH, 6 * D], "in_w": [DEPTH, D, INW], "in_b": [DEPTH, INW],
    "ssd_dt_bias": [DEPTH, 32], "ssd_a_log": [DEPTH, 32], "ssd_dx": [DEPTH, 1024], "ssd_norm_w": [DEPTH, 1024],
    "ssd_out_w": [DEPTH, 1024, 1024], "na_out_w": [DEPTH, 1024, 1024], "pool_w": [DEPTH, 4, 192, 192],
    "pool_out_w": [DEPTH, 768, 1024], "fnet_out_w": [DEPTH, 768, 1024], "mix_out_w": [DEPTH, 1024, 1024],
    "ln1_g": [DEPTH, 1024], "ln1_b": [DEPTH, 1024], "mlp_up_w": [DEPTH, 1024, 4096], "mlp_down_w": [DEPTH, 4096, 1024],
    "mlp_down_b": [DEPTH, 1024], "ln2_g": [DEPTH, 1024], "ln2_b": [DEPTH, 1024],
    "cT": [128, 8, 3], "ada_bT": [DEPTH, 128, 48], "in_bT": [DEPTH, 128, len(FM_TILES)],
    "convwT": [DEPTH, 128, 16, 5], "convbT": [DEPTH, 128, 16], "rpbT": [DEPTH, 16, 64, 15 * 64],
    "pool_scaleT": [DEPTH, 128, 8], "mlp_up_bT": [DEPTH, 128, 32],
    "x_in": [NB, SEQ, D], "ctx_in": [NB, CTX, D],
}


def host_layout(inputs, core, n_layers=DEPTH):
    m = _host_layout(inputs, core)
    if n_layers != DEPTH:
        for k, shp in W_SPECS.items():
            if shp[0] == DEPTH and k != "cT":
                m[k] = np.ascontiguousarray(m[k][:n_layers])
    return m


def _host_layout(inputs, core):
    m = {}
    f = lambda a: np.ascontiguousarray(a, dtype=np.float32)
    for k in ("ada_w", "ada_b", "in_w", "in_b", "ssd_norm_w", "ssd_out_w", "na_out_w", "pool_w", "pool_out_w",
              "fnet_out_w", "mix_out_w", "ln1_g", "ln1_b", "mlp_up_w", "mlp_down_w", "mlp_down_b", "ln2_g", "ln2_b"):
        m[k] = f(inputs[k])
    m["ssd_dt_bias"] = f(inputs["ssd_dt_bias"].reshape(DEPTH, 32))
    m["ssd_a_log"] = f(inputs["ssd_a_log"].reshape(DEPTH, 32))
    m["ssd_dx"] = f(np.repeat(inputs["ssd_d"], 64, axis=1))
    b0 = NB * core
    m["x_in"] = f(inputs["x"][b0:b0 + NB])
    m["ctx_in"] = f(inputs["ctx"][b0:b0 + NB])
    cc = np.stack([inputs["c"][b0], inputs["c"][b0 + 1], inputs["c_ctx"]], -1)
    m["cT"] = f(cc.reshape(8, 128, 3).transpose(1, 0, 2))
    m["ada_bT"] = f(inputs["ada_b"].reshape(DEPTH, 48, 128).transpose(0, 2, 1))
    ib = np.zeros((DEPTH, 128, len(FM_TILES)), np.float32)
    for j, (nm, c0, n, r0) in enumerate(FM_TILES):
        ib[:, :n, j] = inputs["in_b"][:, c0:c0 + n]
    m["in_bT"] = ib
    m["convwT"] = f(inputs["ssd_conv_w"].reshape(DEPTH, 5, 16, 128).transpose(0, 3, 2, 1))
    m["convbT"] = f(inputs["ssd_conv_b"].reshape(DEPTH, 16, 128).transpose(0, 2, 1))
    rpb = inputs["na_rpb"]
    cq = np.arange(64)
    cs = np.clip(cq - 8, 0, 48)
    ck = np.arange(64)
    valid = (ck[:, None] >= cs[None, :]) & (ck[:, None] < cs[None, :] + 16)
    dc = np.clip(ck[:, None] - cq[None, :] + 15, 0, 30)
    g = rpb[:, :, ::-1, :][:, :, :, dc]
    g = np.where(valid[None, None, None], g, np.float32(NEG))
    m["rpbT"] = f(g.transpose(0, 1, 3, 2, 4).reshape(DEPTH, 16, 64, 15 * 64))
    ps = np.zeros((DEPTH, 128, 8), np.float32)
    for gi in range(4):
        ps[:, :128, 2 * gi] = inputs["pool_scale"][:, 192 * gi:192 * gi + 128]
        ps[:, :64, 2 * gi + 1] = inputs["pool_scale"][:, 192 * gi + 128:192 * gi + 192]
    m["pool_scaleT"] = ps
    m["mlp_up_bT"] = f(inputs["mlp_up_b"].reshape(DEPTH, 32, 128).transpose(0, 2, 1))
    return m


class K:
    def __init__(self, n_layers=DEPTH, dump=(), feed=()):
        P = self.P = Prog()
        self.n_layers = n_layers
        self.dump = set(dump)
        self.feed = set(feed)
        self.inp = {}
        for k, shp in W_SPECS.items():
            shp = list(shp)
            if shp[0] == DEPTH and k not in ("cT",):
                shp[0] = n_layers
            self.inp[k] = P.dram(k, shp, F32, kind="ExternalInput").ap()
        for k, (shp, dt) in CONST_SPECS.items():
            self.inp[k] = P.dram(k, shp, dt, kind="ExternalInput").ap()
        self.out = P.dram("out", [NB, SEQ, D], F32, kind="ExternalOutput").ap()
        self.scr = {}
        for nm, shp, dt in (
            ("X", [TT, D], F32), ("XBCT", [2048, TT], F32), ("Z", [TT, 1024], F32), ("DT", [TT, 32], F32),
            ("QT", [1024, TT], BF16), ("KT", [1024, TT], BF16), ("V", [TT, 1024], BF16), ("PLT", [768, TT], F32),
            ("FN", [TT, 768], BF16), ("GT", [4096, TT], BF16), ("YF", [TT, 1024], F32),
            ("YST", [1024, TT], BF16), ("ONT", [1024, TT], BF16), ("PMT", [768, TT], BF16), ("FMT", [768, TT], BF16),
        ):
            kind = "ExternalOutput" if nm in self.dump else ("ExternalInput" if nm in self.feed else "Internal")
            self.scr[nm] = P.dram("scr_" + nm, shp, dt, kind=kind).ap()
        self.ident_bf = P.sbuf("ident_bf_sb", [128, 128], BF16)
        self.ident_f = P.sbuf("ident_f_sb", [128, 128], F32)
        self.modT = P.sbuf("modT", [128, 4, 8, 3], F32)
        self.gbc = P.sbuf("gbc", [128, 3, 1024], F32)
        self.sT = P.sbuf("sT", [128, 8, 3], BF16)
        self.sbc = P.sbuf("sbc", [128, 8, 3, 128], BF16)
        self.t_const = T()
        self.t_mod = T()
        self.ps = P.psum("ps", [128, 8, 512], F32)
        self.tb = [T() for _ in range(8)]
        self.bank_rr = 0
        self.ar = Arena(P, 188 * 1024)
        P.dma("sp", self.ident_bf[:], self.inp["ident_bf"], writes=[self.t_const])
        P.dma("sp", self.ident_f[:], self.inp["ident_f"], writes=[self.t_const])

    def bank(self):
        b = self.bank_rr % 8
        self.bank_rr += 1
        return b

    def psb(self, b):
        return self.ps[:, b, :].bitcast(BF16)

    def phase_mod_init(self):
        P, ar = self.P, self.ar
        ar.reset()
        cT = ar.alloc([128, 8, 3], F32)
        t = T()
        P.dma("sp", cT, self.inp["cT"], writes=[t])
        P.op("act", lambda e: e.activation(self.sT[:], cT, AF.Silu), reads=[t], writes=[self.t_const])
        P.op("dve", lambda e: e.tensor_copy(self.sbc[:], self.sT[:].unsqueeze(3).to_broadcast([128, 8, 3, 128])),
             reads=[self.t_const], writes=[self.t_const])
        P.barrier()

    def phase_mod(self, l, secs):
        P, ar = self.P, self.ar
        ar.reset()
        abT = ar.alloc([128, 48], F32)
        t_ab = T()
        P.dma("sp", abT, self.inp["ada_bT"][l], writes=[t_ab])
        wt = [ar.alloc([128, 8, 512], BF16) for _ in range(2)]
        t_w = [T(), T()]
        bb = [ar.alloc([128, 512], F32) for _ in range(2)]
        t_bb = [T(), T()]
        wsrc = self.inp["ada_w"][l].rearrange("(kc p) c -> p kc c", p=128)
        for j in range(12):
            sec = j // 2
            if sec not in secs:
                continue
            w, tw = wt[j % 2], t_w[j % 2]
            P.dma("pool", w, wsrc[:, :, 512 * j:512 * (j + 1)], writes=[tw])
            if sec in (2, 5):
                half = j % 2
                bbt, tbb = bb[half], t_bb[half]
                P.dma("sp", bbt, self.inp["ada_b"][l, 512 * j:512 * (j + 1)].partition_broadcast(128), writes=[tbb])
                for r in range(3):
                    bk = self.bank()
                    for kc in range(8):
                        P.op("pe", lambda e, bk=bk, kc=kc, r=r, w=w: e.matmul(self.ps[:, bk, :], self.sbc[:, kc, r, :], w[:, kc, :], start=(kc == 0), stop=(kc == 7)),
                             reads=[self.t_const, tw], writes=[self.tb[bk]])
                    P.op("dve", lambda e, bk=bk, r=r, half=half, bbt=bbt: e.tensor_tensor(self.gbc[:, r, 512 * half:512 * (half + 1)], self.ps[:, bk, :], bbt, ALU.add),
                         reads=[self.tb[bk], tbb], writes=[self.t_mod])
            else:
                si = {0: 0, 1: 1, 3: 2, 4: 3}[sec]
                for ct in range(4):
                    bk = self.bank()
                    kci = (j % 2) * 4 + ct
                    for kc in range(8):
                        P.op("pe", lambda e, bk=bk, kc=kc, ct=ct, w=w: e.matmul(self.ps[:, bk, 0:3], w[:, kc, 128 * ct:128 * (ct + 1)], self.sT[:, kc, :], start=(kc == 0), stop=(kc == 7)),
                             reads=[self.t_const, tw], writes=[self.tb[bk]])
                    colj = j * 4 + ct
                    if si in (1, 3):
                        P.op("dve", lambda e, bk=bk, si=si, kci=kci, colj=colj: e.tensor_scalar(self.modT[:, si, kci, :], self.ps[:, bk, 0:3], abT[:, colj:colj + 1], 1.0, ALU.add, ALU.add),
                             reads=[self.tb[bk], t_ab], writes=[self.t_mod])
                    else:
                        P.op("dve", lambda e, bk=bk, si=si, kci=kci, colj=colj: e.tensor_scalar(self.modT[:, si, kci, :], self.ps[:, bk, 0:3], abT[:, colj:colj + 1], None, ALU.add),
                             reads=[self.tb[bk], t_ab], writes=[self.t_mod])
        P.barrier()

    def xsrc(self, l, tile):
        if l > 0:
            return self.scr["X"][128 * tile:128 * (tile + 1), :]
        if tile < 4:
            b, r = tile // 2, tile % 2
            return self.inp["ctx_in"][b, 128 * r:128 * (r + 1), :]
        b, r = (tile - 4) // 16, (tile - 4) % 16
        return self.inp["x_in"][b, 128 * r:128 * (r + 1), :]

    def ln_stats(self, xt, t_x, ntile, mv, t_mv, st, rstd, eps):
        P = self.P
        for i in range(ntile):
            for hh in range(2):
                P.op("dve", lambda e, i=i, hh=hh: e.bn_stats(st[:, i, hh, :], xt[:, i, 512 * hh:512 * (hh + 1)]), reads=[t_x], writes=[t_mv])
            P.op("dve", lambda e, i=i: e.bn_aggr(mv[:, i, :], st[:, i, :, :]), reads=[t_mv], writes=[t_mv])
        P.op("act", lambda e: e.activation(rstd, mv[:, :, 1], AF.Sqrt, bias=eps), reads=[t_mv], writes=[t_mv])
        P.op("dve", lambda e: e.reciprocal(rstd, rstd), reads=[t_mv], writes=[t_mv])

    def lnt_chunk(self, xt, t_x, ch, sec0, hT, t_h, bufs):
        P = self.P
        mv, st, rstd, xn, t_mv, t_xn = bufs
        self.ln_stats(xt, t_x, 4, mv, t_mv, st, rstd, self.eps_ln)
        for i in range(4):
            tile = 4 * ch + i
            row = tile_modrow(tile)
            P.op("dve", lambda e, i=i: e.tensor_scalar(xn[:, i, :], xt[:, i, :], mv[:, i, 0:1], rstd[:, i:i + 1], ALU.subtract, ALU.mult),
                 reads=[t_x, t_mv], writes=[t_xn[i]])
            bk = self.bank()
            pb = self.psb(bk)
            for kc in range(8):
                P.op("pe", lambda e, i=i, kc=kc, pb=pb: e.transpose(pb[:, 128 * kc:128 * (kc + 1)], xn[:, i, 128 * kc:128 * (kc + 1)], self.ident_bf[:]),
                     reads=[t_xn[i], self.t_const], writes=[self.tb[bk]])
            for kc in range(8):
                P.op("act", lambda e, kc=kc, pb=pb, row=row, tile=tile: e.activation(hT[:, kc, 128 * tile:128 * (tile + 1)], pb[:, 128 * kc:128 * (kc + 1)], AF.Identity,
                                                                                   bias=self.modT[:, sec0, kc, row:row + 1], scale=self.modT[:, sec0 + 1, kc, row:row + 1]),
                     reads=[self.tb[bk], self.t_mod], writes=[t_h[ch]])

    def alloc_ln_bufs(self):
        ar = self.ar
        mv = ar.alloc([128, 4, 2], F32)
        st = ar.alloc([128, 4, 2, 6], F32)
        rstd = ar.alloc([128, 4], F32)
        xn = ar.alloc([128, 4, 1024], BF16)
        return (mv, st, rstd, xn, T(), [T() for _ in range(4)])

    def phase_lnt(self, l):
        P, ar = self.P, self.ar
        ar.reset()
        self.hT = ar.alloc([128, 8, TT], BF16)
        self.t_h = [T() for _ in range(NCH)]
        self.arena_keep = ar.off
        xt = [ar.alloc([128, 4, 1024], F32) for _ in range(2)]
        t_x = [T(), T()]
        bufs = self.alloc_ln_bufs()
        for ch in range(NCH):
            x_, tx = xt[ch % 2], t_x[ch % 2]
            for i in range(4):
                P.dma("sp", x_[:, i, :], self.xsrc(l, 4 * ch + i), writes=[tx])
            self.lnt_chunk(x_, tx, ch, 0, self.hT, self.t_h, bufs)
        P.barrier()

    def phase_inproj(self, l):
        P, ar = self.P, self.ar
        ar.off = self.arena_keep
        hT, t_h = self.hT, self.t_h
        nfm = len(FM_TILES)
        ibT = ar.alloc([128, nfm], F32)
        t_ib = T()
        P.dma("sp", ibT, self.inp["in_bT"][l], writes=[t_ib])
        P.op("dve", lambda e: e.tensor_scalar(ibT[:, 16:24], ibT[:, 16:24], 0.125, None, ALU.mult), reads=[t_ib], writes=[t_ib])
        wsrc = self.inp["in_w"][l].rearrange("(kc p) c -> p kc c", p=128)
        wt = [ar.alloc([128, 8, 512], BF16) for _ in range(2)]
        t_w = [T(), T()]
        stg = [ar.alloc([128, TT], F32) for _ in range(2)]
        t_s = [T(), T()]
        wi = 0
        for j, (nm, c0, n, r0) in enumerate(FM_TILES):
            w, tw = wt[wi % 2], t_w[wi % 2]
            wi += 1
            P.dma("pool", w[:, :, 0:n], wsrc[:, :, c0:c0 + n], writes=[tw])
            dst, dt_ = {"xbc": ("XBCT", F32), "q": ("QT", BF16), "k": ("KT", BF16), "pool": ("PLT", F32), "gate": ("GT", BF16)}[nm]
            sg, ts = stg[j % 2], t_s[j % 2]
            sgv = sg if dt_ == F32 else sg.bitcast(BF16)[:, 0:TT]
            for ch in range(NCH):
                bk = self.bank()
                for kc in range(8):
                    P.op("pe", lambda e, bk=bk, kc=kc, ch=ch, w=w, n=n: e.matmul(self.ps[0:n, bk, :], w[:, kc, 0:n], hT[:, kc, 512 * ch:512 * (ch + 1)], start=(kc == 0), stop=(kc == 7)),
                         reads=[tw, t_h[ch]], writes=[self.tb[bk]])
                o = sgv[0:n, 512 * ch:512 * (ch + 1)]
                i_ = self.ps[0:n, bk, :]
                if nm == "gate":
                    P.op("act", lambda e, o=o, i_=i_, j=j, n=n: e.activation(o, i_, AF.Sigmoid, bias=ibT[0:n, j:j + 1]), reads=[self.tb[bk], t_ib], writes=[ts])
                elif nm == "q":
                    P.op("dve", lambda e, o=o, i_=i_, j=j, n=n: e.tensor_scalar(o, i_, 0.125, ibT[0:n, j:j + 1], ALU.mult, ALU.add), reads=[self.tb[bk], t_ib], writes=[ts])
                else:
                    P.op("dve", lambda e, o=o, i_=i_, j=j, n=n: e.tensor_scalar(o, i_, ibT[0:n, j:j + 1], None, ALU.add), reads=[self.tb[bk], t_ib], writes=[ts])
            P.dma("sp", self.scr[dst][r0:r0 + n, :], sgv[0:n, :], reads=[ts])
        bbc = [ar.alloc([128, 512], F32) for _ in range(2)]
        t_bb = [T(), T()]
        tm = [("Z", C_Z, 512, 0, F32), ("Z", C_Z + 512, 512, 512, F32), ("V", C_V, 512, 0, BF16), ("V", C_V + 512, 512, 512, BF16),
              ("FN", C_FN, 512, 0, BF16), ("FN", C_FN + 512, 256, 512, BF16), ("DT", C_DT, 32, 0, F32)]
        for j, (dst, c0, n, d0, dt_) in enumerate(tm):
            w, tw = wt[wi % 2], t_w[wi % 2]
            wi += 1
            P.dma("pool", w[:, :, 0:n], wsrc[:, :, c0:c0 + n], writes=[tw])
            bb_, tbb = bbc[j % 2], t_bb[j % 2]
            P.dma("sp", bb_[:, 0:n], self.inp["in_b"][l, c0:c0 + n].partition_broadcast(128), writes=[tbb])
            for ch in range(NCH):
                sg, ts = stg[ch % 2], t_s[ch % 2]
                sgv = (sg if dt_ == F32 else sg.bitcast(BF16))[:, 0:4 * n].rearrange("p (i c) -> p i c", i=4)
                for i in range(4):
                    tile = 4 * ch + i
                    bk = self.bank()
                    for kc in range(8):
                        P.op("pe", lambda e, bk=bk, kc=kc, tile=tile, w=w, n=n: e.matmul(self.ps[:, bk, 0:n], hT[:, kc, 128 * tile:128 * (tile + 1)], w[:, kc, 0:n], start=(kc == 0), stop=(kc == 7)),
                             reads=[tw, t_h[ch]], writes=[self.tb[bk]])
                    P.op("dve", lambda e, bk=bk, i=i, sgv=sgv, bb_=bb_, n=n: e.tensor_tensor(sgv[:, i, :], self.ps[:, bk, 0:n], bb_[:, 0:n], ALU.add),
                         reads=[self.tb[bk], tbb], writes=[ts])
                P.dma("sp", self.scr[dst][512 * ch:512 * (ch + 1), d0:d0 + n].rearrange("(i p) c -> p i c", p=128), sgv, reads=[ts])
        P.barrier()

    def rbank(self, lo=4, hi=8):
        b = lo + (self.bank_rr % (hi - lo))
        self.bank_rr += 1
        return b

    def phase_ssd(self, l, b, need_ctx):
        P, ar = self.P, self.ar
        ar.reset()
        tiles = seq_tiles(b)
        cbase = [CTX * b, 2 * CTX + SEQ * b]
        BT = ar.alloc([128, 4, TS], BF16)
        CT = ar.alloc([128, 4, TS], BF16)
        xs_tok = ar.alloc([128, 18, 1024], BF16)
        B_tok = ar.alloc([128, 18, 512], BF16)
        dtt = ar.alloc([128, 18, 32], F32)
        dta = ar.alloc([128, 18, 32], F32)
        tri = ar.alloc([128, 2, 128], F32)
        mneg = ar.alloc([128, 2, 128], BF16)
        onehot = ar.alloc([16, 16, 128], F32)
        rotm = ar.alloc([128, 128], F32)
        cw = ar.alloc([128, 16, 5], F32)
        cb_ = ar.alloc([128, 16], F32)
        dtb = ar.alloc([128, 32], F32)
        abc = ar.alloc([128, 32], F32)
        dxbc = ar.alloc([128, 1024], F32)
        nwbc = ar.alloc([128, 1024], F32)
        t_c = T()
        t_BT, t_CT, t_xs, t_Bt, t_dt = T(), T(), T(), T(), T()
        P.dma("sp", tri, self.inp["tri"].rearrange("d s l -> s d l"), writes=[t_c])
        P.dma("sp", mneg, self.inp["maskneg"].rearrange("d s l -> s d l"), writes=[t_c])
        P.dma("sp", onehot, self.inp["onehot"], writes=[t_c])
        P.dma("sp", rotm, self.inp["rotm"], writes=[t_c])
        P.dma("sp", cw, self.inp["convwT"][l], writes=[t_c])
        P.dma("sp", cb_, self.inp["convbT"][l], writes=[t_c])
        P.dma("sp", dtb, self.inp["ssd_dt_bias"][l].partition_broadcast(128), writes=[t_c])
        P.dma("sp", abc, self.inp["ssd_a_log"][l].partition_broadcast(128), writes=[t_c])
        P.dma("sp", dxbc, self.inp["ssd_dx"][l].partition_broadcast(128), writes=[t_c])
        P.dma("sp", nwbc, self.inp["ssd_norm_w"][l].partition_broadcast(128), writes=[t_c])
        mark = ar.off
        BW = 2 + CTX + 4 + SEQ + 2
        buf = ar.alloc([128, BW], F32)
        acc = ar.alloc([128, BW], F32)
        xsT = ar.alloc([128, BW], BF16)
        rc = [ar.alloc([128, 2, 512], F32) for _ in range(2)]
        t1 = ar.alloc([128, 512], F32)
        t2 = ar.alloc([128, 512], F32)
        t_buf, t_acc, t_xsT, t_rc, t_t1, t_t2 = T(), T(), T(), [T(), T()], T(), T()
        P.op("pool", lambda e: e.memset(buf, 0.0), writes=[t_buf])
        XB = self.scr["XBCT"]

        def col_of(ci):
            return 2 + 128 * ci if ci < 2 else 2 + CTX + 4 + 128 * (ci - 2)

        rci = 0
        for ct in range(16):
            ceng = "dve"
            P.dma("sp", buf[:, 2:2 + CTX], XB[128 * ct:128 * (ct + 1), cbase[0]:cbase[0] + CTX], writes=[t_buf])
            P.dma("sp", buf[:, 2 + CTX + 4:2 + CTX + 4 + SEQ], XB[128 * ct:128 * (ct + 1), cbase[1]:cbase[1] + SEQ], writes=[t_buf])
            W_ = BW - 4
            P.op(ceng, lambda e, ct=ct: e.tensor_scalar(acc[:, 2:2 + W_], buf[:, 0:W_], cw[:, ct, 0:1], None, ALU.mult), reads=[t_buf, t_c], writes=[t_acc])
            for k in range(1, 5):
                P.op(ceng, lambda e, ct=ct, k=k: e.scalar_tensor_tensor(acc[:, 2:2 + W_], buf[:, k:k + W_], cw[:, ct, k:k + 1], acc[:, 2:2 + W_], ALU.mult, ALU.add),
                     reads=[t_buf, t_acc, t_c], writes=[t_acc])
            if ct < 8:
                P.op("act", lambda e, ct=ct: e.activation(xsT[:, 2:2 + W_], acc[:, 2:2 + W_], AF.Silu, bias=cb_[:, ct:ct + 1]), reads=[t_acc, t_c], writes=[t_xsT])
                src, tsrc = xsT, t_xsT
                dst3 = xs_tok[:, :, 128 * ct:128 * (ct + 1)]
                tdst = t_xs
                cof = col_of
            else:
                g = (ct - 8) % 4
                isB = ct < 12
                dstT, tdT = (BT, t_BT) if isB else (CT, t_CT)
                P.op("act", lambda e, ct=ct: e.activation(acc[:, 2:2 + W_], acc[:, 2:2 + W_], AF.Silu, bias=cb_[:, ct:ct + 1]), reads=[t_acc, t_c], writes=[t_acc])
                P.op("dve", lambda e, g=g, dstT=dstT: e.tensor_copy(dstT[:, g, 0:CTX], acc[:, 2:2 + CTX]), reads=[t_acc], writes=[tdT])
                for qq in range(4):
                    r_, trc = rc[rci % 2], t_rc[rci % 2]
                    rci += 1
                    P.dma("sp", r_[:, 0, :], self.inp["ropecos"][:, 512 * qq:512 * (qq + 1)], writes=[trc])
                    P.dma("sp", r_[:, 1, :], self.inp["ropesin"][:, 512 * qq:512 * (qq + 1)], writes=[trc])
                    c0 = 2 + CTX + 4 + 512 * qq
                    bk = self.bank()
                    P.op("pe", lambda e, bk=bk, c0=c0: e.matmul(self.ps[:, bk, :], rotm, acc[:, c0:c0 + 512], start=True, stop=True), reads=[t_c, t_acc], writes=[self.tb[bk]])
                    P.op("pool", lambda e, c0=c0, r_=r_: e.tensor_tensor(t1, acc[:, c0:c0 + 512], r_[:, 0, :], ALU.mult), reads=[t_acc, trc], writes=[t_t1])
                    P.op("dve", lambda e, bk=bk, r_=r_: e.tensor_tensor(t2, self.ps[:, bk, :], r_[:, 1, :], ALU.mult), reads=[self.tb[bk], trc], writes=[t_t2])
                    P.op("dve", lambda e, g=g, qq=qq, dstT=dstT: e.tensor_tensor(dstT[:, g, CTX + 512 * qq:CTX + 512 * (qq + 1)], t1, t2, ALU.add), reads=[t_t1, t_t2], writes=[tdT])
                if not isB:
                    continue
                src, tsrc = BT[:, g, :], t_BT
                dst3 = B_tok[:, :, 128 * g:128 * (g + 1)]
                tdst = t_Bt
                cof = lambda ci: 128 * ci
            for c0_ in (0, 8, 16):
                nci = min(8, 18 - c0_)
                bk = self.bank()
                pb = self.psb(bk)
                for ii in range(nci):
                    co = cof(c0_ + ii)
                    P.op("pe", lambda e, pb=pb, ii=ii, co=co, src=src: e.transpose(pb[:, 128 * ii:128 * (ii + 1)], src[:, co:co + 128], self.ident_bf[:]),
                         reads=[tsrc, self.t_const], writes=[self.tb[bk]])
                P.op("act", lambda e, pb=pb, nci=nci, c0_=c0_, dst3=dst3: e.activation(dst3[:, c0_:c0_ + nci, :], pb[:, 0:128 * nci].rearrange("p (a b) -> p a b", a=nci), AF.Identity),
                     reads=[self.tb[bk]], writes=[tdst])
        DTs = self.scr["DT"]
        P.dma("sp", dtt[:, 0:2, :], DTs[cbase[0]:cbase[0] + CTX, :].rearrange("(i p) c -> p i c", p=128), writes=[t_dt])
        P.dma("sp", dtt[:, 2:18, :], DTs[cbase[1]:cbase[1] + SEQ, :].rearrange("(i p) c -> p i c", p=128), writes=[t_dt])
        P.op("dve", lambda e: e.tensor_tensor(dtt, dtt, dtb.unsqueeze(1).to_broadcast([128, 18, 32]), ALU.add), reads=[t_dt, t_c], writes=[t_dt])
        P.op("act", lambda e: e.activation(dtt, dtt, AF.Exp), reads=[t_dt], writes=[t_dt])
        P.op("act", lambda e: e.activation(dtt, dtt, AF.Ln, bias=1.0), reads=[t_dt], writes=[t_dt])
        P.op("act", lambda e: e.activation(abc, abc, AF.Exp), reads=[t_c], writes=[t_c])
        P.op("dve", lambda e: e.scalar_tensor_tensor(dta, dtt, -1.0, abc.unsqueeze(1).to_broadcast([128, 18, 32]), ALU.mult, ALU.mult), reads=[t_dt, t_c], writes=[t_dt])
        P.barrier()
        ar.off = mark
        E = ar.alloc([128, 16, 128], BF16)
        M = ar.alloc([128, 16, 128], BF16)
        cbT = ar.alloc([128, 4, 128], BF16)
        xdt = ar.alloc([128, 1024], BF16)
        xdtw = ar.alloc([128, 1024], BF16)
        xsD = ar.alloc([128, 1024], BF16)
        st = ar.alloc([128, 1024], F32)
        Sbf = ar.alloc([128, 1024], BF16)
        tt_ = ar.alloc([128, 1024], F32)
        yy = ar.alloc([128, 1024], F32)
        yf = ar.alloc([128, 1024], F32)
        zt = ar.alloc([128, 1024], F32)
        sq = ar.alloc([128, 1024], F32)
        ynb = ar.alloc([128, 1024], BF16)
        ysT = ar.alloc([128, 8, 128], BF16)
        negA = ar.alloc([128, 16], F32)
        eA = ar.alloc([128, 16], F32)
        ATs = ar.alloc([16, 128], F32)
        cdec = ar.alloc([128, 16], F32)
        ss = ar.alloc([128, 4], F32)
        t_E, t_M, t_cb, t_xdt, t_xdtw, t_xsD, t_st, t_Sbf, t_tt, t_yy, t_yf, t_z, t_sq, t_ynb, t_ysT = [T() for _ in range(15)]
        t_A, t_cd, t_ss = T(), T(), T()
        t_yfd = [T() for _ in range(18)]
        YF = self.scr["YF"]
        for dr in range(2):
            order = list(range(18)) if dr == 0 else [1, 0] + list(range(17, 1, -1))
            last = 127 if dr == 0 else 0
            for oi, ci in enumerate(order):
                first = oi == 0
                final = oi == 17
                gt = tiles[ci]
                skip_out = (not need_ctx) and ci < 2
                cs = 128 * ci
                dd = dta[:, ci, 16 * dr:16 * dr + 16]
                bkA = self.rbank()
                P.op("pe", lambda e, bkA=bkA, dd=dd, dr=dr: e.matmul(self.ps[:, bkA, 0:16], tri[:, dr, :], dd, start=True, stop=True), reads=[t_c, t_dt], writes=[self.tb[bkA]])
                P.op("pe", lambda e, bkA=bkA, dd=dd, dr=dr: e.matmul(self.ps[0:16, bkA, 128:256], dd, tri[:, dr, :], start=True, stop=True), reads=[t_c, t_dt], writes=[self.tb[bkA]])
                P.op("dve", lambda e, bkA=bkA: e.tensor_scalar(negA, self.ps[:, bkA, 0:16], -1.0, None, ALU.mult), reads=[self.tb[bkA]], writes=[t_A])
                P.op("act", lambda e, bkA=bkA: e.activation(eA, self.ps[:, bkA, 0:16], AF.Exp), reads=[self.tb[bkA]], writes=[t_A])
                P.op("dve", lambda e, bkA=bkA: e.tensor_copy(ATs, self.ps[0:16, bkA, 128:256]), reads=[self.tb[bkA]], writes=[t_A])
                bkC = self.rbank()
                for g in range(4):
                    P.op("pe", lambda e, bkC=bkC, g=g, cs=cs: e.matmul(self.ps[:, bkC, 128 * g:128 * (g + 1)], BT[:, g, cs:cs + 128], CT[:, g, cs:cs + 128], start=True, stop=True),
                         reads=[t_BT, t_CT], writes=[self.tb[bkC]])
                P.op("act", lambda e, bkC=bkC: e.activation(cbT, self.ps[:, bkC, :].rearrange("p (a b) -> p a b", a=4), AF.Identity), reads=[self.tb[bkC]], writes=[t_cb])
                for g in range(4):
                    bkB = self.rbank()
                    for hh in range(4):
                        h = 4 * g + hh
                        P.op("pe", lambda e, bkB=bkB, hh=hh, h=h: e.matmul(self.ps[:, bkB, 128 * hh:128 * (hh + 1)], onehot[:, h, :], ATs, start=True, stop=False), reads=[t_c, t_A], writes=[self.tb[bkB]])
                        P.op("pe", lambda e, bkB=bkB, hh=hh, dr=dr: e.matmul(self.ps[:, bkB, 128 * hh:128 * (hh + 1)], self.ident_bf[:], mneg[:, dr, :], start=False, stop=True), reads=[t_c, self.t_const], writes=[self.tb[bkB]])
                    for hh in range(4):
                        h = 4 * g + hh
                        P.op("act", lambda e, bkB=bkB, hh=hh, h=h: e.activation(E[:, h, :], self.ps[:, bkB, 128 * hh:128 * (hh + 1)], AF.Exp, bias=negA[:, h:h + 1]), reads=[self.tb[bkB], t_A], writes=[t_E])
                    if not final:
                        P.op("act", lambda e, bkB=bkB, g=g, last=last: e.activation(cdec[:, 4 * g:4 * g + 4], self.ps[:, bkB, :].rearrange("p (a b) -> p a b", a=4)[:, :, last], AF.Exp), reads=[self.tb[bkB]], writes=[t_cd])
                    P.op("pool", lambda e, g=g: e.tensor_tensor(M[:, 4 * g:4 * g + 4, :], E[:, 4 * g:4 * g + 4, :], cbT[:, g:g + 1, :].to_broadcast([128, 4, 128]), ALU.mult), reads=[t_E, t_cb], writes=[t_M])
                xs3 = xs_tok[:, ci, :].rearrange("p (h d) -> p h d", h=16)
                P.op("dve", lambda e, xs3=xs3, ci=ci, dr=dr: e.tensor_tensor(xdt.rearrange("p (h d) -> p h d", h=16), xs3, dtt[:, ci, 16 * dr:16 * dr + 16].unsqueeze(2).to_broadcast([128, 16, 64]), ALU.mult),
                     reads=[t_xs, t_dt], writes=[t_xdt])
                if dr == 1 and not skip_out:
                    P.op("pool", lambda e, ci=ci: e.tensor_tensor(xsD, xs_tok[:, ci, :], dxbc, ALU.mult), reads=[t_xs, t_c], writes=[t_xsD])
                if not skip_out:
                    for h in range(16):
                        o = self.ps[:, h // 8, 64 * (h % 8):64 * (h % 8 + 1)]
                        if dr == 1:
                            P.op("pe", lambda e, o=o, h=h: e.matmul(o, self.ident_bf[:], xsD[:, 64 * h:64 * (h + 1)], start=True, stop=False), reads=[t_xsD, self.t_const], writes=[self.tb[h // 8]])
                        P.op("pe", lambda e, o=o, h=h, dr=dr: e.matmul(o, M[:, h, :], xdt[:, 64 * h:64 * (h + 1)], start=(dr == 0), stop=True), reads=[t_M, t_xdt], writes=[self.tb[h // 8]])
                    if not first:
                        for g in range(4):
                            P.op("pe", lambda e, g=g, cs=cs: e.matmul(self.ps[:, 2 + g // 2, 256 * (g % 2):256 * (g % 2 + 1)], CT[:, g, cs:cs + 128], Sbf[:, 256 * g:256 * (g + 1)], start=True, stop=True),
                                 reads=[t_CT, t_Sbf], writes=[self.tb[2 + g // 2]])
                        P.op("dve", lambda e: e.tensor_tensor(tt_.rearrange("p (h d) -> p h d", h=16), self.ps[:, 2:4, :].rearrange("p a (h d) -> p (a h) d", d=64), eA.unsqueeze(2).to_broadcast([128, 16, 64]), ALU.mult),
                             reads=[self.tb[2], self.tb[3], t_A], writes=[t_tt])
                        P.op("dve", lambda e: e.tensor_tensor(yy.rearrange("p (a c) -> p a c", a=2), self.ps[:, 0:2, :], tt_.rearrange("p (a c) -> p a c", a=2), ALU.add),
                             reads=[self.tb[0], self.tb[1], t_tt], writes=[t_yy])
                    else:
                        P.op("dve", lambda e: e.tensor_copy(yy.rearrange("p (a c) -> p a c", a=2), self.ps[:, 0:2, :]), reads=[self.tb[0], self.tb[1]], writes=[t_yy])
                    if dr == 0:
                        P.dma("sp", YF[128 * gt:128 * (gt + 1), :], yy, reads=[t_yy], writes=[t_yfd[ci]])
                    else:
                        P.dma("sp", yf, YF[128 * gt:128 * (gt + 1), :], reads=[t_yfd[ci]], writes=[t_yf])
                        P.dma("sp", zt, self.scr["Z"][128 * gt:128 * (gt + 1), :], writes=[t_z])
                        P.op("pool", lambda e: e.tensor_tensor(yy, yy, yf, ALU.add), reads=[t_yy, t_yf], writes=[t_yy])
                        P.op("act", lambda e: e.activation(zt, zt, AF.Silu), reads=[t_z], writes=[t_z])
                        P.op("pool", lambda e: e.tensor_tensor(yy, yy, zt, ALU.mult), reads=[t_yy, t_z], writes=[t_yy])
                        P.op("pool", lambda e: e.tensor_tensor(sq, yy, yy, ALU.mult), reads=[t_yy], writes=[t_sq])
                        P.op("dve", lambda e: e.reduce_sum(ss, sq.rearrange("p (g c) -> p g c", g=4), mybir.AxisListType.X), reads=[t_sq], writes=[t_ss])
                        P.op("act", lambda e: e.activation(ss, ss, AF.Sqrt, bias=1e-5, scale=1.0 / 256), reads=[t_ss], writes=[t_ss])
                        P.op("dve", lambda e: e.reciprocal(ss, ss), reads=[t_ss], writes=[t_ss])
                        P.op("dve", lambda e: e.tensor_tensor(sq.rearrange("p (g c) -> p g c", g=4), yy.rearrange("p (g c) -> p g c", g=4), ss.unsqueeze(2).to_broadcast([128, 4, 256]), ALU.mult),
                             reads=[t_yy, t_ss], writes=[t_sq])
                        P.op("pool", lambda e: e.tensor_tensor(ynb, sq, nwbc, ALU.mult), reads=[t_sq, t_c], writes=[t_ynb])
                        bk = self.rbank()
                        pb = self.psb(bk)
                        for kc in range(8):
                            P.op("pe", lambda e, pb=pb, kc=kc: e.transpose(pb[:, 128 * kc:128 * (kc + 1)], ynb[:, 128 * kc:128 * (kc + 1)], self.ident_bf[:]), reads=[t_ynb, self.t_const], writes=[self.tb[bk]])
                        P.op("act", lambda e, pb=pb: e.activation(ysT, pb.rearrange("p (a b) -> p a b", a=8), AF.Identity), reads=[self.tb[bk]], writes=[t_ysT])
                        P.dma("sp", self.scr["YST"][:, 128 * gt:128 * (gt + 1)].rearrange("(kc p) c -> p kc c", p=128), ysT, reads=[t_ysT])
                if not final:
                    P.op("pool", lambda e, last=last: e.tensor_tensor(xdtw.rearrange("p (h d) -> p h d", h=16), xdt.rearrange("p (h d) -> p h d", h=16), E[:, :, last:last + 1].to_broadcast([128, 16, 64]), ALU.mult),
                         reads=[t_xdt, t_E], writes=[t_xdtw])
                    for g in range(4):
                        P.op("pe", lambda e, g=g, ci=ci: e.matmul(self.ps[:, 2 + g // 2, 256 * (g % 2):256 * (g % 2 + 1)], B_tok[:, ci, 128 * g:128 * (g + 1)], xdtw[:, 256 * g:256 * (g + 1)], start=True, stop=True),
                             reads=[t_Bt, t_xdtw], writes=[self.tb[2 + g // 2]])
                    if first:
                        P.op("dve", lambda e: e.tensor_copy(st.rearrange("p (a c) -> p a c", a=2), self.ps[:, 2:4, :]), reads=[self.tb[2], self.tb[3]], writes=[t_st])
                    else:
                        P.op("pool", lambda e: e.tensor_tensor(st.rearrange("p (h d) -> p h d", h=16), st.rearrange("p (h d) -> p h d", h=16), cdec.unsqueeze(2).to_broadcast([128, 16, 64]), ALU.mult),
                             reads=[t_st, t_cd], writes=[t_st])
                        P.op("dve", lambda e: e.tensor_tensor(st.rearrange("p (a c) -> p a c", a=2), st.rearrange("p (a c) -> p a c", a=2), self.ps[:, 2:4, :], ALU.add), reads=[t_st, self.tb[2], self.tb[3]], writes=[t_st])
                    P.op("act", lambda e: e.activation(Sbf, st, AF.Identity), reads=[t_st], writes=[t_Sbf])
        P.barrier()

    def phase_na(self, l, b, need_ctx):
        P, ar = self.P, self.ar
        ar.reset()
        cbase = [CTX * b, 2 * CTX + SEQ * b]
        QT = ar.alloc([128, 8, TS], BF16)
        KT = ar.alloc([128, 8, TS], BF16)
        V = ar.alloc([128, 18, 1024], BF16)
        t_q = T()
        for (dst, nm) in ((QT, "QT"), (KT, "KT")):
            src = self.scr[nm].rearrange("(a p) c -> p a c", p=128)
            P.dma("sp", dst[:, :, 0:CTX], src[:, :, cbase[0]:cbase[0] + CTX], writes=[t_q])
            P.dma("sp", dst[:, :, CTX:TS], src[:, :, cbase[1]:cbase[1] + SEQ], writes=[t_q])
        Vs = self.scr["V"]
        P.dma("sp", V[:, 0:2, :], Vs[cbase[0]:cbase[0] + CTX, :].rearrange("(i p) c -> p i c", p=128), writes=[t_q])
        P.dma("sp", V[:, 2:18, :], Vs[cbase[1]:cbase[1] + SEQ, :].rearrange("(i p) c -> p i c", p=128), writes=[t_q])
        ones = ar.alloc([128, 64], BF16)
        P.op("pool", lambda e: e.memset(ones, 1.0), writes=[t_q])
        rp = [ar.alloc([128, 960], BF16) for _ in range(2)]
        t_rp = [T(), T()]
        PTi = [ar.alloc([128, 896], BF16) for _ in range(2)]
        PTe = [ar.alloc([128, 768], BF16) for _ in range(2)]
        t_pi, t_pe = [T(), T()], [T(), T()]
        for i in range(2):
            P.op("pool", lambda e, i=i: e.memset(PTi[i], 0.0), writes=[t_pi[i]])
            P.op("pool", lambda e, i=i: e.memset(rp[i], 0.0), writes=[t_rp[i]])
        oT = [ar.alloc([64, TS], BF16) for _ in range(2)]
        t_o = [T(), T()]
        rec = [ar.alloc([64, 128], F32) for _ in range(2)]
        t_rec = [T(), T()]
        it = 0
        for h in range(16):
            pb = 64 * (h % 2)
            a_ = h // 2
            rp_, trp = rp[h % 2], t_rp[h % 2]
            P.dma("pool", rp_[0:64, :], self.inp["rpbT"][l, h], writes=[trp])
            P.dma("pool", rp_[64:128, 64:960], self.inp["rpbT"][l, h][:, 0:896], writes=[trp])
            o_, to = oT[h % 2], t_o[h % 2]
            Qh = QT[pb:pb + 64, a_, :]
            Kh = KT[pb:pb + 64, a_, :]
            qlist = [("lat", I) for I in range(16)] + ([("ctx", 0), ("ctx", 1)] if need_ctx else [])
            for (kind, I) in qlist:
                par = it % 2
                it += 1
                bkL = 2 * par
                bkM = 4 + par
                psL = self.ps[:, bkL:bkL + 2, :].rearrange("p a c -> p (a c)")
                tL = [self.tb[bkL], self.tb[bkL + 1]]
                psM = self.ps[:, bkM, :]
                tM = self.tb[bkM]
                rec_, trec = rec[par], t_rec[par]
                if kind == "ctx":
                    qc = 128 * I
                    PT, tP = PTe[par], t_pe[par]
                    for c in range(2):
                        P.op("pe", lambda e, c=c, qc=qc, Kh=Kh, Qh=Qh, psM=psM: e.matmul(psM[:, 128 * c:128 * (c + 1)], Kh[:, 128 * c:128 * (c + 1)], Qh[:, qc:qc + 128], start=True, stop=True),
                             reads=[t_q], writes=[tM])
                    P.op("act", lambda e, PT=PT, psM=psM: e.activation(PT[:, 512:768], psM[:, 0:256], AF.Exp), reads=[tM], writes=[tP])
                    pv = [(c, 512 + 128 * c) for c in range(2)]
                else:
                    qc = CTX + 128 * I
                    interior = 2 <= I <= 13
                    PT, tP = (PTi[par], t_pi[par]) if interior else (PTe[par], t_pe[par])
                    lc = 640 if interior else 512
                    if interior:
                        J0 = I - 2
                        nsl = 5
                    else:
                        J0 = 0 if I < 2 else 12
                        nsl = 4
                    cbk = {s_: s_ for s_ in range(nsl)}
                    for s in range(nsl):
                        J = J0 + s
                        drr0 = 7 - 2 * J + 2 * I
                        kc0 = CTX + 128 * J
                        oc = 128 * s
                        o = psL[:, oc:oc + 128]
                        tbk = tL[s // 4]
                        P.op("pe", lambda e, o=o, kc0=kc0, qc=qc, Kh=Kh, Qh=Qh: e.matmul(o, Kh[:, kc0:kc0 + 128], Qh[:, qc:qc + 128], start=True, stop=False),
                             reads=[t_q], writes=[tbk])
                        P.op("pe", lambda e, o=o, drr0=drr0, rp_=rp_: e.matmul(o, self.ident_bf[:], rp_[:, 64 * drr0:64 * drr0 + 128], start=False, stop=True),
                             reads=[trp, self.t_const], writes=[tbk])
                    for c in range(2):
                        P.op("pe", lambda e, c=c, qc=qc, Kh=Kh, Qh=Qh, psM=psM: e.matmul(psM[:, 128 * c:128 * (c + 1)], Kh[:, 128 * c:128 * (c + 1)], Qh[:, qc:qc + 128], start=True, stop=True),
                             reads=[t_q], writes=[tM])
                    P.op("act", lambda e, PT=PT, psL=psL, lc=lc: e.activation(PT[:, 0:lc], psL[:, 0:lc], AF.Exp), reads=tL, writes=[tP])
                    if interior:
                        P.op("pool", lambda e, PT=PT: e.memset(PT[0:64, 64:128], 0.0), reads=[], writes=[tP])
                        P.op("pool", lambda e, PT=PT: e.memset(PT[0:64, 512:576], 0.0), reads=[], writes=[tP])
                        P.op("pool", lambda e, PT=PT: e.memset(PT[64:128, 512:640], 0.0), reads=[], writes=[tP])
                    P.op("act", lambda e, PT=PT, psM=psM, lc=lc: e.activation(PT[:, lc:lc + 256], psM[:, 0:256], AF.Exp), reads=[tM], writes=[tP])
                    pv = [(2 + J0 + s, 128 * cbk[s]) for s in range(nsl)] + [(c, lc + 128 * c) for c in range(2)]
                npv = len(pv)
                for i_, (vt, pc) in enumerate(pv):
                    P.op("pe", lambda e, vt=vt, pc=pc, i_=i_, npv=npv, PT=PT, psM=psM, h=h: e.matmul(psM[0:64, 256:384], V[:, vt, 64 * h:64 * (h + 1)], PT[:, pc:pc + 128], start=(i_ == 0), stop=(i_ == npv - 1)),
                         reads=[t_q, tP], writes=[tM])
                for i_, (vt, pc) in enumerate(pv):
                    P.op("pe", lambda e, pc=pc, i_=i_, npv=npv, PT=PT, psM=psM: e.matmul(psM[0:64, 384:512], ones, PT[:, pc:pc + 128], start=(i_ == 0), stop=(i_ == npv - 1)),
                         reads=[t_q, tP], writes=[tM])
                P.op("dve", lambda e, rec_=rec_, psM=psM: e.reciprocal(rec_, psM[0:64, 384:512]), reads=[tM], writes=[trec])
                oc_ = qc
                P.op("dve", lambda e, rec_=rec_, psM=psM, o_=o_, oc_=oc_: e.tensor_tensor(o_[:, oc_:oc_ + 128], psM[0:64, 256:384], rec_, ALU.mult), reads=[tM, trec], writes=[to])
            ON = self.scr["ONT"]
            if need_ctx:
                P.dma("sp", ON[64 * h:64 * (h + 1), cbase[0]:cbase[0] + CTX], o_[:, 0:CTX], reads=[to])
            P.dma("sp", ON[64 * h:64 * (h + 1), cbase[1]:cbase[1] + SEQ], o_[:, CTX:TS], reads=[to])
        P.barrier()

    def phase_pool(self, l, b, need_ctx):
        P, ar = self.P, self.ar
        ar.reset()
        cbase = [CTX * b, 2 * CTX + SEQ * b]
        PW = 8 + CTX + 16 + SEQ + 8
        o_c, o_l = 8, 8 + CTX + 16
        u = [ar.alloc([128, PW], F32) for _ in range(2)]
        A = ar.alloc([128, PW], F32)
        B = ar.alloc([128, PW], F32)
        inv = ar.alloc([128, PW], F32)
        pl = [ar.alloc([128, TS], BF16) for _ in range(2)]
        pw = ar.alloc([128, 2, 192], BF16)
        psc = ar.alloc([128, 8], F32)
        stg = [ar.alloc([128, TS], BF16) for _ in range(2)]
        t_u, t_A, t_B, t_inv, t_pl, t_pw, t_psc, t_stg = [T(), T()], T(), T(), T(), [T(), T()], T(), T(), [T(), T()]
        for i in range(2):
            P.op("pool", lambda e, i=i: e.memset(u[i], 0.0), writes=[t_u[i]])
        P.dma("sp", psc, self.inp["pool_scaleT"][l], writes=[t_psc])
        PL = self.scr["PLT"]
        PM = self.scr["PMT"]
        si = 0
        for g in range(4):
            nlev = g + 1
            P.dma("sp", inv, self.inp["invcnt"][g].partition_broadcast(128), writes=[t_inv])
            P.dma("pool", pw[:, 0, :], self.inp["pool_w"][l, g, 0:128, :], writes=[t_pw])
            P.dma("pool", pw[0:64, 1, :], self.inp["pool_w"][l, g, 128:192, :], writes=[t_pw])
            for kt in range(2):
                n = 128 if kt == 0 else 64
                r0 = 192 * g + 128 * kt
                eng = "pool" if kt == 0 else "dve"
                u_, tu = u[kt], t_u[kt]
                P.dma("sp", u_[0:n, o_c:o_c + CTX], PL[r0:r0 + n, cbase[0]:cbase[0] + CTX], writes=[tu])
                P.dma("sp", u_[0:n, o_l:o_l + SEQ], PL[r0:r0 + n, cbase[1]:cbase[1] + SEQ], writes=[tu])
                srcb, tsrc = u_, tu
                sh = [(1, 0), (1, 1), (2, 2), (4, 4)]
                lo, hi = 0, PW
                for lv in range(nlev):
                    dstb, tdst = (A, t_A) if lv % 2 == 0 else (B, t_B)
                    s1, s2 = sh[lv]
                    lo2, hi2 = lo + s1, hi - s2
                    P.op(eng, lambda e, dstb=dstb, srcb=srcb, lo2=lo2, hi2=hi2, s1=s1, s2=s2, n=n: e.tensor_tensor(dstb[0:n, lo2:hi2], srcb[0:n, lo2 - s1:hi2 - s1], srcb[0:n, lo2 + s2:hi2 + s2], ALU.add),
                         reads=[tsrc], writes=[tdst])
                    srcb, tsrc, lo, hi = dstb, tdst, lo2, hi2
                oth, toth = (B, t_B) if srcb is A else (A, t_A)
                P.op(eng, lambda e, oth=oth, srcb=srcb, n=n: e.tensor_tensor(oth[0:n, 8:PW - 8], srcb[0:n, 8:PW - 8], inv[0:n, 8:PW - 8], ALU.mult), reads=[tsrc, t_inv], writes=[toth])
                P.op(eng, lambda e, oth=oth, u_=u_, kt=kt, n=n: e.tensor_tensor(pl[kt][0:n, 0:CTX], oth[0:n, o_c:o_c + CTX], u_[0:n, o_c:o_c + CTX], ALU.subtract), reads=[toth, tu], writes=[t_pl[kt]])
                P.op(eng, lambda e, oth=oth, u_=u_, kt=kt, n=n: e.tensor_tensor(pl[kt][0:n, CTX:TS], oth[0:n, o_l:o_l + SEQ], u_[0:n, o_l:o_l + SEQ], ALU.subtract), reads=[toth, tu], writes=[t_pl[kt]])
            for mt in range(2):
                mw = 128 if mt == 0 else 64
                sg, tsg = stg[si % 2], t_stg[si % 2]
                si += 1
                chunks = ([(0, CTX)] if need_ctx else []) + [(CTX + 512 * q, 512) for q in range(4)]
                for (c0, nc_) in chunks:
                    bk = self.bank()
                    for kt in range(2):
                        kw = 128 if kt == 0 else 64
                        P.op("pe", lambda e, bk=bk, kt=kt, kw=kw, mt=mt, mw=mw, c0=c0, nc_=nc_: e.matmul(self.ps[0:mw, bk, 0:nc_], pw[0:kw, kt, 128 * mt:128 * mt + mw], pl[kt][0:kw, c0:c0 + nc_], start=(kt == 0), stop=(kt == 1)),
                             reads=[t_pw, t_pl[kt]], writes=[self.tb[bk]])
                    P.op("act", lambda e, bk=bk, sg=sg, mw=mw, c0=c0, nc_=nc_, g=g, mt=mt: e.activation(sg[0:mw, c0:c0 + nc_], self.ps[0:mw, bk, 0:nc_], AF.Identity, scale=psc[0:mw, 2 * g + mt:2 * g + mt + 1]),
                         reads=[self.tb[bk], t_psc], writes=[tsg])
                r0 = 192 * g + 128 * mt
                if need_ctx:
                    P.dma("sp", PM[r0:r0 + mw, cbase[0]:cbase[0] + CTX], sg[0:mw, 0:CTX], reads=[tsg])
                P.dma("sp", PM[r0:r0 + mw, cbase[1]:cbase[1] + SEQ], sg[0:mw, CTX:TS], reads=[tsg])
        P.barrier()

    def phase_fnet(self, l, b, need_ctx):
        P, ar = self.P, self.ar
        ar.reset()
        cbase = [CTX * b, 2 * CTX + SEQ * b]
        ut = ar.alloc([128, 18, 768], BF16)
        A1 = ar.alloc([128, 8, TS], BF16)
        A2 = ar.alloc([128, 8, TS], BF16)
        cd = ar.alloc([128, 2, 192], BF16)
        sd = ar.alloc([128, 2, 192], BF16)
        cq = [ar.alloc([128, 16, 512], BF16) for _ in range(2)]
        sq_ = [ar.alloc([128, 16, 512], BF16) for _ in range(2)]
        t_u, t_A1, t_A2, t_cd, t_cq = T(), T(), T(), T(), [T(), T()]
        FN = self.scr["FN"]
        if need_ctx:
            P.dma("sp", ut[:, 0:2, :], FN[cbase[0]:cbase[0] + CTX, :].rearrange("(i p) c -> p i c", p=128), writes=[t_u])
        P.dma("sp", ut[:, 2:18, :], FN[cbase[1]:cbase[1] + SEQ, :].rearrange("(i p) c -> p i c", p=128), writes=[t_u])
        for (dst, nm) in ((cd, "dftcd"), (sd, "dftsdn")):
            P.dma("sp", dst[:, 0, :], self.inp[nm][0:128, :], writes=[t_cd])
            P.dma("sp", dst[0:64, 1, :], self.inp[nm][128:192, :], writes=[t_cd])
        ctiles = [(192 * hd + 128 * kt, 128 if kt == 0 else 64) for hd in range(4) for kt in range(2)]
        jobs = [(CTX + 512 * q, 512, 16, 2, "2048", 512 * q) for q in range(4)]
        if need_ctx:
            jobs.append((0, 256, 2, 0, "256", 0))
        for ji, (c0, nc_, na, a0, nm, sc0) in enumerate(jobs):
            c_, s_, tcq = cq[ji % 2], sq_[ji % 2], t_cq[ji % 2]
            P.dma("sp", c_[:, 0:na, 0:nc_], self.inp["dftc" + nm][:, sc0:sc0 + nc_].rearrange("(a p) c -> p a c", p=128), writes=[tcq])
            P.dma("sp", s_[:, 0:na, 0:nc_], self.inp["dfts" + nm][:, sc0:sc0 + nc_].rearrange("(a p) c -> p a c", p=128), writes=[tcq])
            for ti, (ch0, w) in enumerate(ctiles):
                for (mat, dstA, tA) in ((c_, A1, t_A1), (s_, A2, t_A2)):
                    bk = self.bank()
                    for a in range(na):
                        P.op("pe", lambda e, bk=bk, a=a, a0=a0, na=na, ch0=ch0, w=w, mat=mat, nc_=nc_: e.matmul(self.ps[0:w, bk, 0:nc_], ut[:, a0 + a, ch0:ch0 + w], mat[:, a, 0:nc_], start=(a == 0), stop=(a == na - 1)),
                             reads=[t_u, tcq], writes=[self.tb[bk]])
                    if mat is c_:
                        P.op("act", lambda e, bk=bk, dstA=dstA, ti=ti, w=w, c0=c0, nc_=nc_: e.activation(dstA[0:w, ti, c0:c0 + nc_], self.ps[0:w, bk, 0:nc_], AF.Identity), reads=[self.tb[bk]], writes=[tA])
                    else:
                        P.op("dve", lambda e, bk=bk, dstA=dstA, ti=ti, w=w, c0=c0, nc_=nc_: e.tensor_copy(dstA[0:w, ti, c0:c0 + nc_], self.ps[0:w, bk, 0:nc_]), reads=[self.tb[bk]], writes=[tA])
        stg = [ar.alloc([128, TS], BF16) for _ in range(2)]
        t_stg = [T(), T()]
        FM = self.scr["FMT"]
        si = 0
        chunks = ([(0, CTX)] if need_ctx else []) + [(CTX + 512 * q, 512) for q in range(4)]
        for hd in range(4):
            for mt in range(2):
                mw = 128 if mt == 0 else 64
                sg, tsg = stg[si % 2], t_stg[si % 2]
                si += 1
                for (c0, nc_) in chunks:
                    bk = self.bank()
                    k = 0
                    for (mat, srcA, tA) in ((cd, A1, t_A1), (sd, A2, t_A2)):
                        for kt in range(2):
                            kw = 128 if kt == 0 else 64
                            P.op("pe", lambda e, bk=bk, k=k, kt=kt, kw=kw, mt=mt, mw=mw, mat=mat, srcA=srcA, hd=hd, c0=c0, nc_=nc_: e.matmul(self.ps[0:mw, bk, 0:nc_], mat[0:kw, kt, 128 * mt:128 * mt + mw], srcA[0:kw, 2 * hd + kt, c0:c0 + nc_], start=(k == 0), stop=(k == 3)),
                                 reads=[t_cd, tA], writes=[self.tb[bk]])
                            k += 1
                    P.op("act", lambda e, bk=bk, sg=sg, mw=mw, c0=c0, nc_=nc_: e.activation(sg[0:mw, c0:c0 + nc_], self.ps[0:mw, bk, 0:nc_], AF.Identity), reads=[self.tb[bk]], writes=[tsg])
                r0 = 192 * hd + 128 * mt
                if need_ctx:
                    P.dma("sp", FM[r0:r0 + mw, cbase[0]:cbase[0] + CTX], sg[0:mw, 0:CTX], reads=[tsg])
                P.dma("sp", FM[r0:r0 + mw, cbase[1]:cbase[1] + SEQ], sg[0:mw, CTX:TS], reads=[tsg])
        P.barrier()

    def resid_ln(self, l, ntile, tiles, psview, xt, t_x, gi, gam, bet, t_w, bufs, addb, eng2="pool"):
        P = self.P
        mv, st, rstd, tmp, t_mv, t_tmp = bufs
        for i in range(ntile):
            row = tile_modrow(tiles[i])
            gb = self.gbc[:, row, :].rearrange("p (a c) -> p a c", a=2)
            tv = tmp[:, i, :].rearrange("p (a c) -> p a c", a=2)
            if addb is not None:
                P.op("dve", lambda e, i=i, tv=tv: e.tensor_tensor(tv, psview(i), addb.rearrange("p (a c) -> p a c", a=2), ALU.add), reads=self.ps_reads(i) + [t_w], writes=[t_tmp[i]])
                P.op(eng2, lambda e, i=i, row=row: e.tensor_tensor(tmp[:, i, :], tmp[:, i, :], self.gbc[:, row, :], ALU.mult), reads=[t_tmp[i], self.t_mod], writes=[t_tmp[i]])
            else:
                P.op("dve", lambda e, i=i, tv=tv, gb=gb: e.tensor_tensor(tv, psview(i), gb, ALU.mult), reads=self.ps_reads(i) + [self.t_mod], writes=[t_tmp[i]])
            P.op("dve", lambda e, i=i: e.scalar_tensor_tensor(xt[:, i, :], xt[:, i, :], ALPHA, tmp[:, i, :], ALU.mult, ALU.add), reads=[t_x, t_tmp[i]], writes=[t_x])
        self.ln_stats(xt, t_x, ntile, mv, t_mv, st, rstd, 1e-6)
        for i in range(ntile):
            P.op("dve", lambda e, i=i: e.tensor_scalar(xt[:, i, :], xt[:, i, :], mv[:, i, 0:1], rstd[:, i:i + 1], ALU.subtract, ALU.mult), reads=[t_x, t_mv], writes=[t_x])
            P.op(eng2, lambda e, i=i: e.tensor_tensor(xt[:, i, :], xt[:, i, :], gam, ALU.mult), reads=[t_x, t_w], writes=[t_x])
            P.op(eng2, lambda e, i=i: e.tensor_tensor(xt[:, i, :], xt[:, i, :], bet, ALU.add), reads=[t_x, t_w], writes=[t_x])

    def phase_merge(self, l, need_ctx):
        P, ar = self.P, self.ar
        ar.reset()
        Wssd = ar.alloc([128, 8, 1024], BF16)
        Wna = ar.alloc([128, 8, 1024], BF16)
        Wpl = ar.alloc([128, 8, 1024], BF16)
        Wfn = ar.alloc([128, 8, 1024], BF16)
        Wmx = ar.alloc([128, 8, 1024], BF16)
        gam = ar.alloc([128, 1024], F32)
        bet = ar.alloc([128, 1024], F32)
        t_w = T()
        P.dma("pool", Wssd, self.inp["ssd_out_w"][l].rearrange("(a p) c -> p a c", p=128), writes=[t_w])
        P.dma("pool", Wna, self.inp["na_out_w"][l].rearrange("(a p) c -> p a c", p=128), writes=[t_w])
        P.dma("pool", Wmx, self.inp["mix_out_w"][l].rearrange("(a p) c -> p a c", p=128), writes=[t_w])
        for (W_, nm) in ((Wpl, "pool_out_w"), (Wfn, "fnet_out_w")):
            for g in range(4):
                P.dma("pool", W_[:, 2 * g, :], self.inp[nm][l, 192 * g:192 * g + 128, :], writes=[t_w])
                P.dma("pool", W_[0:64, 2 * g + 1, :], self.inp[nm][l, 192 * g + 128:192 * g + 192, :], writes=[t_w])
        P.dma("sp", gam, self.inp["ln1_g"][l].partition_broadcast(128), writes=[t_w])
        P.dma("sp", bet, self.inp["ln1_b"][l].partition_broadcast(128), writes=[t_w])
        br = ar.alloc([128, 8, 512], BF16)
        gt = ar.alloc([128, 8, 512], BF16)
        mg = ar.alloc([128, 8, 512], F32)
        tm = ar.alloc([128, 512], F32)
        mgb = ar.alloc([128, 8, 512], BF16)
        xt = ar.alloc([128, 4, 1024], F32)
        tmp = ar.alloc([128, 4, 1024], F32)
        mv = ar.alloc([128, 4, 2], F32)
        st = ar.alloc([128, 4, 2, 6], F32)
        rstd = ar.alloc([128, 4], F32)
        t_br, t_gt, t_mg, t_tm, t_mgb, t_x = T(), T(), T(), T(), T(), T()
        bufs = (mv, st, rstd, tmp, T(), [T() for _ in range(4)])
        ktl = [(192 * g + 128 * kt, 128 if kt == 0 else 64) for g in range(4) for kt in range(2)]
        specs = [("YST", Wssd, [(128 * a, 128) for a in range(8)]), ("ONT", Wna, [(128 * a, 128) for a in range(8)]),
                 ("PMT", Wpl, ktl), ("FMT", Wfn, ktl)]
        for ch in range(0 if need_ctx else 1, NCH):
            cs = 512 * ch
            for i in range(4):
                P.dma("sp", xt[:, i, :], self.xsrc(l, 4 * ch + i), writes=[t_x])
            for bi, (nm, W_, kts) in enumerate(specs):
                src = self.scr[nm]
                if nm in ("YST", "ONT"):
                    P.dma("sp", br[:, 0:8, :], src[:, cs:cs + 512].rearrange("(a p) c -> p a c", p=128), writes=[t_br])
                else:
                    for ti, (r0, n) in enumerate(kts):
                        P.dma("sp", br[0:n, ti, :], src[r0:r0 + n, cs:cs + 512], writes=[t_br])
                P.dma("sp", gt, self.scr["GT"][1024 * bi:1024 * (bi + 1), cs:cs + 512].rearrange("(a p) c -> p a c", p=128), writes=[t_gt])
                nk = len(kts)
                for dt_ in range(8):
                    bk = self.bank()
                    for ti, (r0, n) in enumerate(kts):
                        P.op("pe", lambda e, bk=bk, ti=ti, n=n, nk=nk, W_=W_, dt_=dt_: e.matmul(self.ps[:, bk, :], W_[0:n, ti, 128 * dt_:128 * (dt_ + 1)], br[0:n, ti, :], start=(ti == 0), stop=(ti == nk - 1)),
                             reads=[t_w, t_br], writes=[self.tb[bk]])
                    if bi == 0:
                        P.op("dve", lambda e, bk=bk, dt_=dt_: e.tensor_tensor(mg[:, dt_, :], self.ps[:, bk, :], gt[:, dt_, :], ALU.mult), reads=[self.tb[bk], t_gt], writes=[t_mg])
                    else:
                        P.op("dve", lambda e, bk=bk, dt_=dt_: e.tensor_tensor(tm, self.ps[:, bk, :], gt[:, dt_, :], ALU.mult), reads=[self.tb[bk], t_gt], writes=[t_tm])
                        P.op("pool", lambda e, dt_=dt_: e.tensor_tensor(mg[:, dt_, :], mg[:, dt_, :], tm, ALU.add), reads=[t_tm, t_mg], writes=[t_mg])
            P.op("act", lambda e: e.activation(mgb, mg, AF.Identity), reads=[t_mg], writes=[t_mgb])
            pbk = []
            for i in range(4):
                b0 = 2 * (i % 4)
                pbk.append(b0)
                for half in range(2):
                    for kc in range(8):
                        P.op("pe", lambda e, b0=b0, half=half, kc=kc, i=i: e.matmul(self.ps[:, b0 + half, :], mgb[:, kc, 128 * i:128 * (i + 1)], Wmx[:, kc, 512 * half:512 * (half + 1)], start=(kc == 0), stop=(kc == 7)),
                             reads=[t_mgb, t_w], writes=[self.tb[b0 + half]])
            self.ps_reads = lambda i: [self.tb[2 * i], self.tb[2 * i + 1]]
            self.resid_ln(l, 4, [4 * ch + i for i in range(4)], lambda i: self.ps[:, 2 * i:2 * i + 2, :], xt, t_x, 0, gam, bet, t_w, bufs, None)
            for i in range(4):
                P.dma("sp", self.scr["X"][128 * (4 * ch + i):128 * (4 * ch + i + 1), :], xt[:, i, :], reads=[t_x])
        P.barrier()

    def phase_mlp(self, l, need_ctx, final):
        P, ar = self.P, self.ar
        ar.reset()
        Wup = ar.alloc([128, 8, 4096], BF16)
        Wdn = ar.alloc([128, 32, 1024], BF16)
        ubT = ar.alloc([128, 32], F32)
        gam = ar.alloc([128, 1024], F32)
        bet = ar.alloc([128, 1024], F32)
        dnb = ar.alloc([128, 1024], F32)
        t_w = T()
        for q in range(4):
            P.dma("pool", Wup[:, :, 1024 * q:1024 * (q + 1)], self.inp["mlp_up_w"][l][:, 1024 * q:1024 * (q + 1)].rearrange("(a p) c -> p a c", p=128), writes=[t_w])
            P.dma("pool", Wdn[:, 8 * q:8 * (q + 1), :], self.inp["mlp_down_w"][l][1024 * q:1024 * (q + 1), :].rearrange("(a p) c -> p a c", p=128), writes=[t_w])
        P.dma("sp", ubT, self.inp["mlp_up_bT"][l], writes=[t_w])
        P.dma("sp", gam, self.inp["ln2_g"][l].partition_broadcast(128), writes=[t_w])
        P.dma("sp", bet, self.inp["ln2_b"][l].partition_broadcast(128), writes=[t_w])
        P.dma("sp", dnb, self.inp["mlp_down_b"][l].partition_broadcast(128), writes=[t_w])
        NT = 2
        h2 = ar.alloc([128, 8, 128 * NT], BF16)
        a2 = ar.alloc([128, 32, 128 * NT], BF16)
        rl = [ar.alloc([128, 128 * NT], BF16) for _ in range(2)]
        xt = ar.alloc([128, NT, 1024], F32)
        tmp = ar.alloc([128, NT, 1024], F32)
        xn = ar.alloc([128, NT, 1024], BF16)
        mv = ar.alloc([128, NT, 2], F32)
        st = ar.alloc([128, NT, 2, 6], F32)
        rstd = ar.alloc([128, NT], F32)
        t_h2, t_a2, t_rl, t_x, t_xn, t_mv = T(), T(), [T(), T()], T(), [T() for _ in range(NT)], T()
        bufs = (mv, st, rstd, tmp, t_mv, [T() for _ in range(NT)])
        first_tile = 0 if need_ctx else 4
        for c2 in range(first_tile // NT, NTILE // NT):
            tiles = [NT * c2 + i for i in range(NT)]
            for i in range(NT):
                P.dma("sp", xt[:, i, :], self.scr["X"][128 * tiles[i]:128 * (tiles[i] + 1), :], writes=[t_x])
            self.ln_stats(xt, t_x, NT, mv, t_mv, st, rstd, 1e-6)
            for i in range(NT):
                row = tile_modrow(tiles[i])
                P.op("dve", lambda e, i=i: e.tensor_scalar(xn[:, i, :], xt[:, i, :], mv[:, i, 0:1], rstd[:, i:i + 1], ALU.subtract, ALU.mult), reads=[t_x, t_mv], writes=[t_xn[i]])
                bk = self.bank()
                pb = self.psb(bk)
                for kc in range(8):
                    P.op("pe", lambda e, i=i, kc=kc, pb=pb: e.transpose(pb[:, 128 * kc:128 * (kc + 1)], xn[:, i, 128 * kc:128 * (kc + 1)], self.ident_bf[:]), reads=[t_xn[i], self.t_const], writes=[self.tb[bk]])
                for kc in range(8):
                    P.op("act", lambda e, kc=kc, pb=pb, row=row, i=i: e.activation(h2[:, kc, 128 * i:128 * (i + 1)], pb[:, 128 * kc:128 * (kc + 1)], AF.Identity, bias=self.modT[:, 2, kc, row:row + 1], scale=self.modT[:, 3, kc, row:row + 1]),
                         reads=[self.tb[bk], self.t_mod], writes=[t_h2])
            for ft in range(32):
                bk = self.bank()
                for kc in range(8):
                    P.op("pe", lambda e, bk=bk, kc=kc, ft=ft: e.matmul(self.ps[:, bk, 0:128 * NT], Wup[:, kc, 128 * ft:128 * (ft + 1)], h2[:, kc, :], start=(kc == 0), stop=(kc == 7)), reads=[t_w, t_h2], writes=[self.tb[bk]])
                r_, tr = rl[ft % 2], t_rl[ft % 2]
                P.op("act", lambda e, bk=bk, ft=ft, r_=r_: e.activation(r_, self.ps[:, bk, 0:128 * NT], AF.Relu, bias=ubT[:, ft:ft + 1]), reads=[self.tb[bk], t_w], writes=[tr])
                P.op("pool", lambda e, ft=ft, r_=r_: e.tensor_tensor(a2[:, ft, :], r_, r_, ALU.mult), reads=[tr], writes=[t_a2])
            for i in range(NT):
                for half in range(2):
                    bk = 2 * i + half
                    for ft in range(32):
                        P.op("pe", lambda e, bk=bk, ft=ft, i=i, half=half: e.matmul(self.ps[:, bk, :], a2[:, ft, 128 * i:128 * (i + 1)], Wdn[:, ft, 512 * half:512 * (half + 1)], start=(ft == 0), stop=(ft == 31)),
                             reads=[t_a2, t_w], writes=[self.tb[bk]])
            self.ps_reads = lambda i: [self.tb[2 * i], self.tb[2 * i + 1]]
            self.resid_ln(l, NT, tiles, lambda i: self.ps[:, 2 * i:2 * i + 2, :], xt, t_x, 1, gam, bet, t_w, bufs, dnb)
            for i in range(NT):
                t = tiles[i]
                if final:
                    bb, r = (t - 4) // 16, (t - 4) % 16
                    P.dma("sp", self.out[bb, 128 * r:128 * (r + 1), :], xt[:, i, :], reads=[t_x])
                else:
                    P.dma("sp", self.scr["X"][128 * t:128 * (t + 1), :], xt[:, i, :], reads=[t_x])
        P.barrier()


def build(n_layers=DEPTH, dump=(), upto=None, last=DEPTH - 1):
    k = K(n_layers, dump)
    k.eps_ln = 1e-6
    P = k.P
    k.phase_mod_init()
    phases = ["mod", "lnt", "inproj", "ssd", "na", "pool", "fnet", "merge", "mlp"]
    for l in range(n_layers):
        need_ctx = l < last
        final = l == last
        def on(nm):
            return upto is None or l < n_layers - 1 or phases.index(nm) <= phases.index(upto)
        if on("mod"):
            k.phase_mod(l, (0, 1, 2, 3, 4))
        if on("lnt"):
            k.phase_lnt(l)
        if on("inproj"):
            k.phase_inproj(l)
        for b in range(NB):
            if on("ssd"):
                k.phase_ssd(l, b, need_ctx)
        for b in range(NB):
            if on("na"):
                k.phase_na(l, b, need_ctx)
        for b in range(NB):
            if on("pool"):
                k.phase_pool(l, b, need_ctx)
            if on("fnet"):
                k.phase_fnet(l, b, need_ctx)
        if on("merge"):
            k.phase_merge(l, need_ctx)
        if on("mlp"):
            k.phase_mod(l, (5,))
            k.phase_mlp(l, need_ctx, final)
        P.barrier(new_epoch=True)
    P.emit()
    P.close()
    return k


_CACHE = {}


def kernel(**inputs):
    inputs = {k: np.asarray(v) for k, v in inputs.items()}
    consts = host_consts()
    if "k" not in _CACHE:
        _CACHE["k"] = build()
    k = _CACHE["k"]
    in_maps = []
    for core in range(8):
        m = host_layout(inputs, core)
        m.update(consts)
        in_maps.append(m)
    res = run_bass_kernel_spmd(k.P.nc, in_maps, core_ids=list(range(8)))
    out = np.concatenate([np.asarray(r["out"]) for r in res.results], axis=0)
    return out.astype(np.float32)
```

```python
import math
import numpy as np
import ml_dtypes
import concourse.bass as bass
import concourse.mybir as mybir
from concourse.bass_utils import run_bass_kernel_spmd
from contextlib import ExitStack

F32 = mybir.dt.float32
BF16 = mybir.dt.bfloat16
AF = mybir.ActivationFunctionType
ALU = mybir.AluOpType

D = 1024
NB = 2
SEQ = 2048
CTX = 256
TS = SEQ + CTX
TT = NB * TS
NTILE = TT // 128
NCH = TT // 512
DEPTH = 4
INW = 11808
NEG = -30000.0
ALPHA = (2 * DEPTH) ** 0.25

C_Z, C_XBC, C_DT, C_Q, C_K, C_V, C_POOL, C_FN, C_G = 0, 1024, 3072, 3104, 4128, 5152, 6176, 6944, 7712


def fm_tiles():
    t = []
    for i in range(16):
        t.append(("xbc", C_XBC + 128 * i, 128, 128 * i))
    for i in range(8):
        t.append(("q", C_Q + 128 * i, 128, 128 * i))
    for i in range(8):
        t.append(("k", C_K + 128 * i, 128, 128 * i))
    for g in range(4):
        t.append(("pool", C_POOL + 192 * g, 128, 192 * g))
        t.append(("pool", C_POOL + 192 * g + 128, 64, 192 * g + 128))
    for i in range(32):
        t.append(("gate", C_G + 128 * i, 128, 128 * i))
    return t


FM_TILES = fm_tiles()


def seq_tiles(b):
    return [2 * b, 2 * b + 1] + [4 + 16 * b + i for i in range(16)]


def tile_modrow(t):
    if t < 4:
        return 2
    return (t - 4) // 16


class T:
    __slots__ = ("lw", "rs")

    def __init__(self):
        self.lw = None
        self.rs = []


class Prog:
    ENG = ("pe", "act", "dve", "pool", "sp")

    def __init__(self, n_dma_sems=40):
        self.nc = bass.Bass("TRN2", target_bir_lowering=False)
        self.q = {e: [] for e in self.ENG}
        self.es = ExitStack()
        self.n_dma_sems = n_dma_sems
        self.dma_cnt = [0] * n_dma_sems
        self.dma_last = [None] * n_dma_sems
        self.dma_next = 0
        self.dma_rr = {}
        self.epoch = 0
        self.nops = 0

    def sbuf(self, name, shape, dt):
        return self.es.enter_context(self.nc.sbuf_tensor(name, list(shape), dt))

    def psum(self, name, shape, dt):
        return self.es.enter_context(self.nc.psum_tensor(name, list(shape), dt))

    def dram(self, name, shape, dt, kind="Internal"):
        return self.nc.dram_tensor(name, list(shape), dt, kind=kind)

    def _deps(self, eng, reads, writes, is_dma):
        idx = len(self.q[eng])
        me = (eng, idx)
        deps = set()
        for t in reads:
            if t.lw is not None:
                deps.add((t.lw, 0))
            t.rs.append(me)
        for t in writes:
            if t.lw is not None:
                deps.add((t.lw, 1))
            for r in t.rs:
                if r != me:
                    deps.add((r, 2))
            t.lw = me
            t.rs = []
        out = set()
        for (p, kind) in deps:
            if p == me:
                continue
            prec = self.q[p[0]][p[1]]
            if prec["dma"] is not None or is_dma:
                out.add(p)
            elif p[0] == eng:
                if kind == 0 and eng != "pe":
                    out.add(p)
            else:
                out.add(p)
        return out

    def _slot(self, eng):
        if eng == "sp":
            lo, hi = 0, self.n_dma_sems - 12
        else:
            lo, hi = self.n_dma_sems - 12, self.n_dma_sems
        c = self.dma_rr.get(eng, 0)
        self.dma_rr[eng] = c + 1
        return lo + c % (hi - lo)

    def op(self, eng, fn, reads=(), writes=()):
        deps = self._deps(eng, reads, writes, False)
        self.q[eng].append({"fn": fn, "deps": deps, "dma": None, "sig": False, "ep": self.epoch})
        self.nops += 1

    def dma(self, eng, out, in_, reads=(), writes=()):
        deps = self._deps(eng, reads, writes, True)
        slot = self._slot(eng)
        prev = self.dma_cnt[slot]
        self.dma_cnt[slot] += 16
        self.dma_last[slot] = (eng, len(self.q[eng]))
        self.q[eng].append({"fn": (out, in_), "deps": deps, "dma": (slot, prev, prev + 16), "sig": False, "ep": self.epoch})
        self.nops += 1

    def dma_fn(self, eng, fn, reads=(), writes=()):
        deps = self._deps(eng, reads, writes, True)
        slot = self._slot(eng)
        prev = self.dma_cnt[slot]
        self.dma_cnt[slot] += 16
        self.dma_last[slot] = (eng, len(self.q[eng]))
        self.q[eng].append({"fn": fn, "deps": deps, "dma": (slot, prev, prev + 16), "sig": False, "ep": self.epoch})
        self.nops += 1

    def barrier(self, new_epoch=False):
        last = {e: len(self.q[e]) - 1 for e in self.ENG}
        for e in self.ENG:
            deps = set()
            for e2 in self.ENG:
                if e2 != e and last[e2] >= 0:
                    deps.add((e2, last[e2]))
            for s in range(self.n_dma_sems):
                if self.dma_last[s] is not None:
                    deps.add(self.dma_last[s])
            self.q[e].append({"fn": None, "deps": deps, "dma": None, "sig": False, "ep": self.epoch})
        if new_epoch:
            self.epoch += 1

    def emit(self):
        nc = self.nc
        q = self.q
        for e in self.ENG:
            for rec in q[e]:
                for (pe_, pi) in rec["deps"]:
                    prec = q[pe_][pi]
                    if prec["fn"] is None:
                        j = pi
                        while j >= 0 and q[pe_][j]["fn"] is None:
                            j -= 1
                        if j >= 0:
                            q[pe_][j]["sig"] = True
                    else:
                        prec["sig"] = True
        nep = self.epoch + 1
        for e in self.ENG:
            c = [0] * nep
            for rec in q[e]:
                if rec["fn"] is not None and rec["dma"] is None and rec["sig"]:
                    c[rec["ep"]] += 1
                    rec["cnt"] = c[rec["ep"]]
        es = self.es
        esem = {(e, k): es.enter_context(nc.semaphore("s_%s_%d" % (e, k))) for e in self.ENG for k in range(nep)}
        dsem = [es.enter_context(nc.semaphore("d%d" % i)) for i in range(self.n_dma_sems)]
        block = es.enter_context(nc.Block())

        def resolve(pe_, pi):
            prec = q[pe_][pi]
            if prec["fn"] is None:
                j = pi
                while j >= 0 and q[pe_][j]["fn"] is None:
                    j -= 1
                if j < 0:
                    return None
                prec = q[pe_][j]
            return prec

        def run(ename, eobj):
            waited = {}
            for rec in q[ename]:
                need = {}
                for (pe_, pi) in rec["deps"]:
                    prec = resolve(pe_, pi)
                    if prec is None:
                        continue
                    if prec["dma"] is not None:
                        key = ("d", prec["dma"][0])
                        val = prec["dma"][2]
                    else:
                        key = ("e", pe_, prec["ep"])
                        val = prec["cnt"]
                    if need.get(key, 0) < val:
                        need[key] = val
                if rec["dma"] is not None:
                    slot, prev, tgt = rec["dma"]
                    if prev > 0:
                        key = ("d", slot)
                        if need.get(key, 0) < prev:
                            need[key] = prev
                for key, val in need.items():
                    if waited.get(key, 0) >= val:
                        continue
                    waited[key] = val
                    sem = dsem[key[1]] if key[0] == "d" else esem[(key[1], key[2])]
                    eobj.wait_ge(sem, val)
                if rec["fn"] is None:
                    continue
                if rec["dma"] is not None:
                    if callable(rec["fn"]):
                        rec["fn"](eobj).then_inc(dsem[rec["dma"][0]], 16)
                    else:
                        out, in_ = rec["fn"]
                        eobj.dma_start(out=out, in_=in_).then_inc(dsem[rec["dma"][0]], 16)
                else:
                    ins = rec["fn"](eobj)
                    if rec["sig"]:
                        ins.then_inc(esem[(ename, rec["ep"])], 1)
            if ename == "sp":
                for i in range(self.n_dma_sems):
                    if self.dma_cnt[i] > 0 and waited.get(("d", i), 0) < self.dma_cnt[i]:
                        eobj.wait_ge(dsem[i], self.dma_cnt[i])

        @block.tensor
        def _(e):
            run("pe", e)

        @block.scalar
        def _(e):
            run("act", e)

        @block.vector
        def _(e):
            run("dve", e)

        @block.gpsimd
        def _(e):
            run("pool", e)

        @block.sync
        def _(e):
            run("sp", e)

    def close(self):
        self.es.close()


class Arena:
    def __init__(self, P, nbytes):
        self.t = P.sbuf("arena", [128, nbytes // 4], F32)
        self.nbytes = nbytes
        self.off = 0

    def reset(self):
        self.off = 0

    def alloc(self, shape, dt):
        esz = 2 if dt == BF16 else 4
        n = 1
        for s in shape[1:]:
            n *= s
        nb = (n * esz + 31) // 32 * 32
        assert self.off + nb <= self.nbytes, ("arena overflow", self.off, nb, self.nbytes)
        a = self.t[:, self.off // 4:(self.off + nb) // 4]
        self.off += nb
        if dt == BF16:
            a = a.bitcast(BF16)
        a = a[:, 0:n]
        if len(shape) == 3:
            a = a.rearrange("p (a b) -> p a b", a=shape[1])
        elif len(shape) == 4:
            a = a.rearrange("p (a b c) -> p a b c", a=shape[1], b=shape[2])
        elif len(shape) == 5:
            a = a.rearrange("p (a b c d) -> p a b c d", a=shape[1], b=shape[2], c=shape[3])
        if shape[0] < 128:
            a = a[0:shape[0]]
        return a


def host_consts():
    c = {}
    bf = ml_dtypes.bfloat16
    c["ident_bf"] = np.eye(128, dtype=np.float32).astype(bf)
    c["ident_f"] = np.eye(128, dtype=np.float32)
    s = np.arange(128)[:, None]
    l = np.arange(128)[None, :]
    c["tri"] = np.stack([(s <= l), (s >= l)]).astype(np.float32)
    c["maskneg"] = np.stack([np.where(s <= l, 0.0, NEG), np.where(s >= l, 0.0, NEG)]).astype(np.float32).astype(bf)
    oh = np.zeros((16, 16, 128), np.float32)
    for h in range(16):
        oh[h, h, :] = 1.0
    c["onehot"] = oh.transpose(1, 0, 2).copy()
    rm = np.zeros((128, 128), np.float32)
    for i in range(64):
        rm[i + 64, i] = -1.0
        rm[i, i + 64] = 1.0
    c["rotm"] = rm
    pos = np.arange(SEQ)
    inv = 10000.0 ** (-np.arange(32, dtype=np.float64) / 32)
    ang = np.concatenate([(pos // 64)[:, None] * inv, (pos % 64)[:, None] * inv], -1)
    c["ropecos"] = np.concatenate([np.cos(ang), np.cos(ang)], -1).T.astype(np.float32).copy()
    c["ropesin"] = np.concatenate([np.sin(ang), np.sin(ang)], -1).T.astype(np.float32).copy()
    PW = 8 + CTX + 16 + SEQ + 8
    ic = np.zeros((4, PW), np.float32)
    for gi, win in enumerate((2, 4, 8, 16)):
        for (n, off) in ((CTX, 8), (SEQ, 8 + CTX + 16)):
            p = np.arange(n)
            lo = np.clip(p - win // 2, 0, n)
            hi = np.clip(p + win - win // 2, 0, n)
            ic[gi, off:off + n] = 1.0 / (hi - lo)
    c["invcnt"] = ic
    for n, nm in ((SEQ, "2048"), (CTX, "256")):
        k = np.arange(n, dtype=np.float64)
        a = 2 * np.pi * np.outer(k, k) / n
        c["dftc" + nm] = (np.cos(a) / math.sqrt(n)).astype(np.float32).astype(bf)
        c["dfts" + nm] = (np.sin(a) / math.sqrt(n)).astype(np.float32).astype(bf)
    k = np.arange(192, dtype=np.float64)
    a = 2 * np.pi * np.outer(k, k) / 192
    c["dftcd"] = (np.cos(a) / math.sqrt(192)).astype(np.float32).astype(bf)
    c["dftsdn"] = (-np.sin(a) / math.sqrt(192)).astype(np.float32).astype(bf)
    return c


CONST_SPECS = {
    "ident_bf": ([128, 128], BF16), "ident_f": ([128, 128], F32), "tri": ([2, 128, 128], F32),
    "maskneg": ([2, 128, 128], BF16), "onehot": ([16, 16, 128], F32), "rotm": ([128, 128], F32),
    "ropecos": ([128, 2048], F32), "ropesin": ([128, 2048], F32), "invcnt": ([4, 8 + CTX + 16 + SEQ + 8], F32),
    "dftc2048": ([2048, 2048], BF16), "dfts2048": ([2048, 2048], BF16), "dftc256": ([256, 256], BF16),
    "dfts256": ([256, 256], BF16), "dftcd": ([192, 192], BF16), "dftsdn": ([192, 192], BF16),
}

W_SPECS = {
    "ada_w": [DEPTH, D, 6 * D], "ada_b": [DEPTH, 6 * D], "in_w": [DEPTH, D, INW], "in_b": [DEPTH, INW],
    "ssd_dt_bias": [DEPTH, 32], "ssd_a_log": [DEPTH, 32], "ssd_dx": [DEPTH, 1024], "ssd_norm_w": [DEPTH, 1024],
    "ssd_out_w": [DEPTH, 1024, 1024], "na_out_w": [DEPTH, 1024, 1024], "pool_w": [DEPTH, 4, 192, 192],
    "pool_out_w": [DEPTH, 768, 1024], "fnet_out_w": [DEPTH, 768, 1024], "mix_out_w": [DEPTH, 1024, 1024],
    "ln1_g": [DEPTH, 1024], "ln1_b": [DEPTH, 1024], "mlp_up_w": [DEPTH, 1024, 4096], "mlp_down_w": [DEPTH, 4096, 1024],
    "mlp_down_b": [DEPTH, 1024], "ln2_g": [DEPTH, 1024], "ln2_b": [DEPTH, 1024],
    "cT": [128, 8, 3], "ada_bT": [DEPTH, 128, 48], "in_bT": [DEPTH, 128, len(FM_TILES)],
    "convwT": [DEPTH, 128, 16, 5], "convbT": [DEPTH, 128, 16], "rpbT": [DEPTH, 16, 64, 15 * 64],
    "pool_scaleT": [DEPTH, 128, 8], "mlp_up_bT": [DEPTH, 128, 32],
    "x_in": [NB, SEQ, D], "ctx_in": [NB, CTX, D],
}


def host_layout(inputs, core, n_layers=DEPTH):
    m = _host_layout(inputs, core)
    if n_layers != DEPTH:
        for k, shp in W_SPECS.items():
            if shp[0] == DEPTH and k != "cT":
                m[k] = np.ascontiguousarray(m[k][:n_layers])
    return m


def _host_layout(inputs, core):
    m = {}
    f = lambda a: np.ascontiguousarray(a, dtype=np.float32)
    for k in ("ada_w", "ada_b", "in_w", "in_b", "ssd_norm_w", "ssd_out_w", "na_out_w", "pool_w", "pool_out_w",
              "fnet_out_w", "mix_out_w", "ln1_g", "ln1_b", "mlp_up_w", "mlp_down_w", "mlp_down_b", "ln2_g", "ln2_b"):
        m[k] = f(inputs[k])
    m["ssd_dt_bias"] = f(inputs["ssd_dt_bias"].reshape(DEPTH, 32))
    m["ssd_a_log"] = f(inputs["ssd_a_log"].reshape(DEPTH, 32))
    m["ssd_dx"] = f(np.repeat(inputs["ssd_d"], 64, axis=1))
    b0 = NB * core
    m["x_in"] = f(inputs["x"][b0:b0 + NB])
    m["ctx_in"] = f(inputs["ctx"][b0:b0 + NB])
    cc = np.stack([inputs["c"][b0], inputs["c"][b0 + 1], inputs["c_ctx"]], -1)
    m["cT"] = f(cc.reshape(8, 128, 3).transpose(1, 0, 2))
    m["ada_bT"] = f(inputs["ada_b"].reshape(DEPTH, 48, 128).transpose(0, 2, 1))
    ib = np.zeros((DEPTH, 128, len(FM_TILES)), np.float32)
    for j, (nm, c0, n, r0) in enumerate(FM_TILES):
        ib[:, :n, j] = inputs["in_b"][:, c0:c0 + n]
    m["in_bT"] = ib
    m["convwT"] = f(inputs["ssd_conv_w"].reshape(DEPTH, 5, 16, 128).transpose(0, 3, 2, 1))
    m["convbT"] = f(inputs["ssd_conv_b"].reshape(DEPTH, 16, 128).transpose(0, 2, 1))
    rpb = inputs["na_rpb"]
    cq = np.arange(64)
    cs = np.clip(cq - 8, 0, 48)
    ck = np.arange(64)
    valid = (ck[:, None] >= cs[None, :]) & (ck[:, None] < cs[None, :] + 16)
    dc = np.clip(ck[:, None] - cq[None, :] + 15, 0, 30)
    g = rpb[:, :, ::-1, :][:, :, :, dc]
    g = np.where(valid[None, None, None], g, np.float32(NEG))
    m["rpbT"] = f(g.transpose(0, 1, 3, 2, 4).reshape(DEPTH, 16, 64, 15 * 64))
    ps = np.zeros((DEPTH, 128, 8), np.float32)
    for gi in range(4):
        ps[:, :128, 2 * gi] = inputs["pool_scale"][:, 192 * gi:192 * gi + 128]
        ps[:, :64, 2 * gi + 1] = inputs["pool_scale"][:, 192 * gi + 128:192 * gi + 192]
    m["pool_scaleT"] = ps
    m["mlp_up_bT"] = f(inputs["mlp_up_b"].reshape(DEPTH, 32, 128).transpose(0, 2, 1))
    return m


class K:
    def __init__(self, n_layers=DEPTH, dump=(), feed=()):
        P = self.P = Prog()
        self.n_layers = n_layers
        self.dump = set(dump)
        self.feed = set(feed)
        self.inp = {}
        for k, shp in W_SPECS.items():
            shp = list(shp)
            if shp[0] == DEPTH and k not in ("cT",):
                shp[0] = n_layers
            self.inp[k] = P.dram(k, shp, F32, kind="ExternalInput").ap()
        for k, (shp, dt) in CONST_SPECS.items():
            self.inp[k] = P.dram(k, shp, dt, kind="ExternalInput").ap()
        self.out = P.dram("out", [NB, SEQ, D], F32, kind="ExternalOutput").ap()
        self.scr = {}
        for nm, shp, dt in (
            ("X", [TT, D], F32), ("XBCT", [2048, TT], F32), ("Z", [TT, 1024], F32), ("DT", [TT, 32], F32),
            ("QT", [1024, TT], BF16), ("KT", [1024, TT], BF16), ("V", [TT, 1024], BF16), ("PLT", [768, TT], F32),
            ("FN", [TT, 768], BF16), ("GT", [4096, TT], BF16), ("YF", [TT, 1024], F32),
            ("YST", [1024, TT], BF16), ("ONT", [1024, TT], BF16), ("PMT", [768, TT], BF16), ("FMT", [768, TT], BF16),
        ):
            kind = "ExternalOutput" if nm in self.dump else ("ExternalInput" if nm in self.feed else "Internal")
            self.scr[nm] = P.dram("scr_" + nm, shp, dt, kind=kind).ap()
        self.ident_bf = P.sbuf("ident_bf_sb", [128, 128], BF16)
        self.ident_f = P.sbuf("ident_f_sb", [128, 128], F32)
        self.modT = P.sbuf("modT", [128, 4, 8, 3], F32)
        self.gbc = P.sbuf("gbc", [128, 3, 1024], F32)
        self.sT = P.sbuf("sT", [128, 8, 3], BF16)
        self.sbc = P.sbuf("sbc", [128, 8, 3, 128], BF16)
        self.t_const = T()
        self.t_mod = T()
        self.ps = P.psum("ps", [128, 8, 512], F32)
        self.tb = [T() for _ in range(8)]
        self.bank_rr = 0
        self.ar = Arena(P, 188 * 1024)
        P.dma("sp", self.ident_bf[:], self.inp["ident_bf"], writes=[self.t_const])
        P.dma("sp", self.ident_f[:], self.inp["ident_f"], writes=[self.t_const])

    def bank(self):
        b = self.bank_rr % 8
        self.bank_rr += 1
        return b

    def psb(self, b):
        return self.ps[:, b, :].bitcast(BF16)

    def phase_mod_init(self):
        P, ar = self.P, self.ar
        ar.reset()
        cT = ar.alloc([128, 8, 3], F32)
        t = T()
        P.dma("sp", cT, self.inp["cT"], writes=[t])
        P.op("act", lambda e: e.activation(self.sT[:], cT, AF.Silu), reads=[t], writes=[self.t_const])
        P.op("dve", lambda e: e.tensor_copy(self.sbc[:], self.sT[:].unsqueeze(3).to_broadcast([128, 8, 3, 128])),
             reads=[self.t_const], writes=[self.t_const])
        P.barrier()

    def phase_mod(self, l, secs):
        P, ar = self.P, self.ar
        ar.reset()
        abT = ar.alloc([128, 48], F32)
        t_ab = T()
        P.dma("sp", abT, self.inp["ada_bT"][l], writes=[t_ab])
        wt = [ar.alloc([128, 8, 512], BF16) for _ in range(2)]
        t_w = [T(), T()]
        bb = [ar.alloc([128, 512], F32) for _ in range(2)]
        t_bb = [T(), T()]
        wsrc = self.inp["ada_w"][l].rearrange("(kc p) c -> p kc c", p=128)
        for j in range(12):
            sec = j // 2
            if sec not in secs:
                continue
            w, tw = wt[j % 2], t_w[j % 2]
            P.dma("pool", w, wsrc[:, :, 512 * j:512 * (j + 1)], writes=[tw])
            if sec in (2, 5):
                half = j % 2
                bbt, tbb = bb[half], t_bb[half]
                P.dma("sp", bbt, self.inp["ada_b"][l, 512 * j:512 * (j + 1)].partition_broadcast(128), writes=[tbb])
                for r in range(3):
                    bk = self.bank()
                    for kc in range(8):
                        P.op("pe", lambda e, bk=bk, kc=kc, r=r, w=w: e.matmul(self.ps[:, bk, :], self.sbc[:, kc, r, :], w[:, kc, :], start=(kc == 0), stop=(kc == 7)),
                             reads=[self.t_const, tw], writes=[self.tb[bk]])
                    P.op("dve", lambda e, bk=bk, r=r, half=half, bbt=bbt: e.tensor_tensor(self.gbc[:, r, 512 * half:512 * (half + 1)], self.ps[:, bk, :], bbt, ALU.add),
                         reads=[self.tb[bk], tbb], writes=[self.t_mod])
            else:
                si = {0: 0, 1: 1, 3: 2, 4: 3}[sec]
                for ct in range(4):
                    bk = self.bank()
                    kci = (j % 2) * 4 + ct
                    for kc in range(8):
                        P.op("pe", lambda e, bk=bk, kc=kc, ct=ct, w=w: e.matmul(self.ps[:, bk, 0:3], w[:, kc, 128 * ct:128 * (ct + 1)], self.sT[:, kc, :], start=(kc == 0), stop=(kc == 7)),
                             reads=[self.t_const, tw], writes=[self.tb[bk]])
                    colj = j * 4 + ct
                    if si in (1, 3):
                        P.op("dve", lambda e, bk=bk, si=si, kci=kci, colj=colj: e.tensor_scalar(self.modT[:, si, kci, :], self.ps[:, bk, 0:3], abT[:, colj:colj + 1], 1.0, ALU.add, ALU.add),
                             reads=[self.tb[bk], t_ab], writes=[self.t_mod])
                    else:
                        P.op("dve", lambda e, bk=bk, si=si, kci=kci, colj=colj: e.tensor_scalar(self.modT[:, si, kci, :], self.ps[:, bk, 0:3], abT[:, colj:colj + 1], None, ALU.add),
                             reads=[self.tb[bk], t_ab], writes=[self.t_mod])
        P.barrier()

    def xsrc(self, l, tile):
        if l > 0:
            return self.scr["X"][128 * tile:128 * (tile + 1), :]
        if tile < 4:
            b, r = tile // 2, tile % 2
            return self.inp["ctx_in"][b, 128 * r:128 * (r + 1), :]
        b, r = (tile - 4) // 16, (tile - 4) % 16
        return self.inp["x_in"][b, 128 * r:128 * (r + 1), :]

    def ln_stats(self, xt, t_x, ntile, mv, t_mv, st, rstd, eps):
        P = self.P
        for i in range(ntile):
            for hh in range(2):
                P.op("dve", lambda e, i=i, hh=hh: e.bn_stats(st[:, i, hh, :], xt[:, i, 512 * hh:512 * (hh + 1)]), reads=[t_x], writes=[t_mv])
            P.op("dve", lambda e, i=i: e.bn_aggr(mv[:, i, :], st[:, i, :, :]), reads=[t_mv], writes=[t_mv])
        P.op("act", lambda e: e.activation(rstd, mv[:, :, 1], AF.Sqrt, bias=eps), reads=[t_mv], writes=[t_mv])
        P.op("dve", lambda e: e.reciprocal(rstd, rstd), reads=[t_mv], writes=[t_mv])

    def lnt_chunk(self, xt, t_x, ch, sec0, hT, t_h, bufs):
        P = self.P
        mv, st, rstd, xn, t_mv, t_xn = bufs
        self.ln_stats(xt, t_x, 4, mv, t_mv, st, rstd, self.eps_ln)
        for i in range(4):
            tile = 4 * ch + i
            row = tile_modrow(tile)
            P.op("dve", lambda e, i=i: e.tensor_scalar(xn[:, i, :], xt[:, i, :], mv[:, i, 0:1], rstd[:, i:i + 1], ALU.subtract, ALU.mult),
                 reads=[t_x, t_mv], writes=[t_xn[i]])
            bk = self.bank()
            pb = self.psb(bk)
            for kc in range(8):
                P.op("pe", lambda e, i=i, kc=kc, pb=pb: e.transpose(pb[:, 128 * kc:128 * (kc + 1)], xn[:, i, 128 * kc:128 * (kc + 1)], self.ident_bf[:]),
                     reads=[t_xn[i], self.t_const], writes=[self.tb[bk]])
            for kc in range(8):
                P.op("act", lambda e, kc=kc, pb=pb, row=row, tile=tile: e.activation(hT[:, kc, 128 * tile:128 * (tile + 1)], pb[:, 128 * kc:128 * (kc + 1)], AF.Identity,
                                                                                   bias=self.modT[:, sec0, kc, row:row + 1], scale=self.modT[:, sec0 + 1, kc, row:row + 1]),
                     reads=[self.tb[bk], self.t_mod], writes=[t_h[ch]])

    def alloc_ln_bufs(self):
        ar = self.ar
        mv = ar.alloc([128, 4, 2], F32)
        st = ar.alloc([128, 4, 2, 6], F32)
        rstd = ar.alloc([128, 4], F32)
        xn = ar.alloc([128, 4, 1024], BF16)
        return (mv, st, rstd, xn, T(), [T() for _ in range(4)])

    def phase_lnt(self, l):
        P, ar = self.P, self.ar
        ar.reset()
        self.hT = ar.alloc([128, 8, TT], BF16)
        self.t_h = [T() for _ in range(NCH)]
        self.arena_keep = ar.off
        xt = [ar.alloc([128, 4, 1024], F32) for _ in range(2)]
        t_x = [T(), T()]
        bufs = self.alloc_ln_bufs()
        for ch in range(NCH):
            x_, tx = xt[ch % 2], t_x[ch % 2]
            for i in range(4):
                P.dma("sp", x_[:, i, :], self.xsrc(l, 4 * ch + i), writes=[tx])
            self.lnt_chunk(x_, tx, ch, 0, self.hT, self.t_h, bufs)
        P.barrier()

    def phase_inproj(self, l):
        P, ar = self.P, self.ar
        ar.off = self.arena_keep
        hT, t_h = self.hT, self.t_h
        nfm = len(FM_TILES)
        ibT = ar.alloc([128, nfm], F32)
        t_ib = T()
        P.dma("sp", ibT, self.inp["in_bT"][l], writes=[t_ib])
        P.op("dve", lambda e: e.tensor_scalar(ibT[:, 16:24], ibT[:, 16:24], 0.125, None, ALU.mult), reads=[t_ib], writes=[t_ib])
        wsrc = self.inp["in_w"][l].rearrange("(kc p) c -> p kc c", p=128)
        wt = [ar.alloc([128, 8, 512], BF16) for _ in range(2)]
        t_w = [T(), T()]
        stg = [ar.alloc([128, TT], F32) for _ in range(2)]
        t_s = [T(), T()]
        wi = 0
        for j, (nm, c0, n, r0) in enumerate(FM_TILES):
            w, tw = wt[wi % 2], t_w[wi % 2]
            wi += 1
            P.dma("pool", w[:, :, 0:n], wsrc[:, :, c0:c0 + n], writes=[tw])
            dst, dt_ = {"xbc": ("XBCT", F32), "q": ("QT", BF16), "k": ("KT", BF16), "pool": ("PLT", F32), "gate": ("GT", BF16)}[nm]
            sg, ts = stg[j % 2], t_s[j % 2]
            sgv = sg if dt_ == F32 else sg.bitcast(BF16)[:, 0:TT]
            for ch in range(NCH):
                bk = self.bank()
                for kc in range(8):
                    P.op("pe", lambda e, bk=bk, kc=kc, ch=ch, w=w, n=n: e.matmul(self.ps[0:n, bk, :], w[:, kc, 0:n], hT[:, kc, 512 * ch:512 * (ch + 1)], start=(kc == 0), stop=(kc == 7)),
                         reads=[tw, t_h[ch]], writes=[self.tb[bk]])
                o = sgv[0:n, 512 * ch:512 * (ch + 1)]
                i_ = self.ps[0:n, bk, :]
                if nm == "gate":
                    P.op("act", lambda e, o=o, i_=i_, j=j, n=n: e.activation(o, i_, AF.Sigmoid, bias=ibT[0:n, j:j + 1]), reads=[self.tb[bk], t_ib], writes=[ts])
                elif nm == "q":
                    P.op("dve", lambda e, o=o, i_=i_, j=j, n=n: e.tensor_scalar(o, i_, 0.125, ibT[0:n, j:j + 1], ALU.mult, ALU.add), reads=[self.tb[bk], t_ib], writes=[ts])
                else:
                    P.op("dve", lambda e, o=o, i_=i_, j=j, n=n: e.tensor_scalar(o, i_, ibT[0:n, j:j + 1], None, ALU.add), reads=[self.tb[bk], t_ib], writes=[ts])
            P.dma("sp", self.scr[dst][r0:r0 + n, :], sgv[0:n, :], reads=[ts])
        bbc = [ar.alloc([128, 512], F32) for _ in range(2)]
        t_bb = [T(), T()]
        tm = [("Z", C_Z, 512, 0, F32), ("Z", C_Z + 512, 512, 512, F32), ("V", C_V, 512, 0, BF16), ("V", C_V + 512, 512, 512, BF16),
              ("FN", C_FN, 512, 0, BF16), ("FN", C_FN + 512, 256, 512, BF16), ("DT", C_DT, 32, 0, F32)]
        for j, (dst, c0, n, d0, dt_) in enumerate(tm):
            w, tw = wt[wi % 2], t_w[wi % 2]
            wi += 1
            P.dma("pool", w[:, :, 0:n], wsrc[:, :, c0:c0 + n], writes=[tw])
            bb_, tbb = bbc[j % 2], t_bb[j % 2]
            P.dma("sp", bb_[:, 0:n], self.inp["in_b"][l, c0:c0 + n].partition_broadcast(128), writes=[tbb])
            for ch in range(NCH):
                sg, ts = stg[ch % 2], t_s[ch % 2]
                sgv = (sg if dt_ == F32 else sg.bitcast(BF16))[:, 0:4 * n].rearrange("p (i c) -> p i c", i=4)
                for i in range(4):
                    tile = 4 * ch + i
                    bk = self.bank()
                    for kc in range(8):
                        P.op("pe", lambda e, bk=bk, kc=kc, tile=tile, w=w, n=n: e.matmul(self.ps[:, bk, 0:n], hT[:, kc, 128 * tile:128 * (tile + 1)], w[:, kc, 0:n], start=(kc == 0), stop=(kc == 7)),
                             reads=[tw, t_h[ch]], writes=[self.tb[bk]])
                    P.op("dve", lambda e, bk=bk, i=i, sgv=sgv, bb_=bb_, n=n: e.tensor_tensor(sgv[:, i, :], self.ps[:, bk, 0:n], bb_[:, 0:n], ALU.add),
                         reads=[self.tb[bk], tbb], writes=[ts])
                P.dma("sp", self.scr[dst][512 * ch:512 * (ch + 1), d0:d0 + n].rearrange("(i p) c -> p i c", p=128), sgv, reads=[ts])
        P.barrier()

    def rbank(self, lo=4, hi=8):
        b = lo + (self.bank_rr % (hi - lo))
        self.bank_rr += 1
        return b

    def phase_ssd(self, l, b, need_ctx):
        P, ar = self.P, self.ar
        ar.reset()
        tiles = seq_tiles(b)
        cbase = [CTX * b, 2 * CTX + SEQ * b]
        BT = ar.alloc([128, 4, TS], BF16)
        CT = ar.alloc([128, 4, TS], BF16)
        xs_tok = ar.alloc([128, 18, 1024], BF16)
        B_tok = ar.alloc([128, 18, 512], BF16)
        dtt = ar.alloc([128, 18, 32], F32)
        dta = ar.alloc([128, 18, 32], F32)
        tri = ar.alloc([128, 2, 128], F32)
        mneg = ar.alloc([128, 2, 128], BF16)
        onehot = ar.alloc([16, 16, 128], F32)
        rotm = ar.alloc([128, 128], F32)
        cw = ar.alloc([128, 16, 5], F32)
        cb_ = ar.alloc([128, 16], F32)
        dtb = ar.alloc([128, 32], F32)
        abc = ar.alloc([128, 32], F32)
        dxbc = ar.alloc([128, 1024], F32)
        nwbc = ar.alloc([128, 1024], F32)
        t_c = T()
        t_BT, t_CT, t_xs, t_Bt, t_dt = T(), T(), T(), T(), T()
        P.dma("sp", tri, self.inp["tri"].rearrange("d s l -> s d l"), writes=[t_c])
        P.dma("sp", mneg, self.inp["maskneg"].rearrange("d s l -> s d l"), writes=[t_c])
        P.dma("sp", onehot, self.inp["onehot"], writes=[t_c])
        P.dma("sp", rotm, self.inp["rotm"], writes=[t_c])
        P.dma("sp", cw, self.inp["convwT"][l], writes=[t_c])
        P.dma("sp", cb_, self.inp["convbT"][l], writes=[t_c])
        P.dma("sp", dtb, self.inp["ssd_dt_bias"][l].partition_broadcast(128), writes=[t_c])
        P.dma("sp", abc, self.inp["ssd_a_log"][l].partition_broadcast(128), writes=[t_c])
        P.dma("sp", dxbc, self.inp["ssd_dx"][l].partition_broadcast(128), writes=[t_c])
        P.dma("sp", nwbc, self.inp["ssd_norm_w"][l].partition_broadcast(128), writes=[t_c])
        mark = ar.off
        BW = 2 + CTX + 4 + SEQ + 2
        buf = ar.alloc([128, BW], F32)
        acc = ar.alloc([128, BW], F32)
        xsT = ar.alloc([128, BW], BF16)
        rc = [ar.alloc([128, 2, 512], F32) for _ in range(2)]
        t1 = ar.alloc([128, 512], F32)
        t2 = ar.alloc([128, 512], F32)
        t_buf, t_acc, t_xsT, t_rc, t_t1, t_t2 = T(), T(), T(), [T(), T()], T(), T()
        P.op("pool", lambda e: e.memset(buf, 0.0), writes=[t_buf])
        XB = self.scr["XBCT"]

        def col_of(ci):
            return 2 + 128 * ci if ci < 2 else 2 + CTX + 4 + 128 * (ci - 2)

        rci = 0
        for ct in range(16):
            ceng = "dve"
            P.dma("sp", buf[:, 2:2 + CTX], XB[128 * ct:128 * (ct + 1), cbase[0]:cbase[0] + CTX], writes=[t_buf])
            P.dma("sp", buf[:, 2 + CTX + 4:2 + CTX + 4 + SEQ], XB[128 * ct:128 * (ct + 1), cbase[1]:cbase[1] + SEQ], writes=[t_buf])
            W_ = BW - 4
            P.op(ceng, lambda e, ct=ct: e.tensor_scalar(acc[:, 2:2 + W_], buf[:, 0:W_], cw[:, ct, 0:1], None, ALU.mult), reads=[t_buf, t_c], writes=[t_acc])
            for k in range(1, 5):
                P.op(ceng, lambda e, ct=ct, k=k: e.scalar_tensor_tensor(acc[:, 2:2 + W_], buf[:, k:k + W_], cw[:, ct, k:k + 1], acc[:, 2:2 + W_], ALU.mult, ALU.add),
                     reads=[t_buf, t_acc, t_c], writes=[t_acc])
            if ct < 8:
                P.op("act", lambda e, ct=ct: e.activation(xsT[:, 2:2 + W_], acc[:, 2:2 + W_], AF.Silu, bias=cb_[:, ct:ct + 1]), reads=[t_acc, t_c], writes=[t_xsT])
                src, tsrc = xsT, t_xsT
                dst3 = xs_tok[:, :, 128 * ct:128 * (ct + 1)]
                tdst = t_xs
                cof = col_of
            else:
                g = (ct - 8) % 4
                isB = ct < 12
                dstT, tdT = (BT, t_BT) if isB else (CT, t_CT)
                P.op("act", lambda e, ct=ct: e.activation(acc[:, 2:2 + W_], acc[:, 2:2 + W_], AF.Silu, bias=cb_[:, ct:ct + 1]), reads=[t_acc, t_c], writes=[t_acc])
                P.op("dve", lambda e, g=g, dstT=dstT: e.tensor_copy(dstT[:, g, 0:CTX], acc[:, 2:2 + CTX]), reads=[t_acc], writes=[tdT])
                for qq in range(4):
                    r_, trc = rc[rci % 2], t_rc[rci % 2]
                    rci += 1
                    P.dma("sp", r_[:, 0, :], self.inp["ropecos"][:, 512 * qq:512 * (qq + 1)], writes=[trc])
                    P.dma("sp", r_[:, 1, :], self.inp["ropesin"][:, 512 * qq:512 * (qq + 1)], writes=[trc])
                    c0 = 2 + CTX + 4 + 512 * qq
                    bk = self.bank()
                    P.op("pe", lambda e, bk=bk, c0=c0: e.matmul(self.ps[:, bk, :], rotm, acc[:, c0:c0 + 512], start=True, stop=True), reads=[t_c, t_acc], writes=[self.tb[bk]])
                    P.op("pool", lambda e, c0=c0, r_=r_: e.tensor_tensor(t1, acc[:, c0:c0 + 512], r_[:, 0, :], ALU.mult), reads=[t_acc, trc], writes=[t_t1])
                    P.op("dve", lambda e, bk=bk, r_=r_: e.tensor_tensor(t2, self.ps[:, bk, :], r_[:, 1, :], ALU.mult), reads=[self.tb[bk], trc], writes=[t_t2])
                    P.op("dve", lambda e, g=g, qq=qq, dstT=dstT: e.tensor_tensor(dstT[:, g, CTX + 512 * qq:CTX + 512 * (qq + 1)], t1, t2, ALU.add), reads=[t_t1, t_t2], writes=[tdT])
                if not isB:
                    continue
                src, tsrc = BT[:, g, :], t_BT
                dst3 = B_tok[:, :, 128 * g:128 * (g + 1)]
                tdst = t_Bt
                cof = lambda ci: 128 * ci
            for c0_ in (0, 8, 16):
                nci = min(8, 18 - c0_)
                bk = self.bank()
                pb = self.psb(bk)
                for ii in range(nci):
                    co = cof(c0_ + ii)
                    P.op("pe", lambda e, pb=pb, ii=ii, co=co, src=src: e.transpose(pb[:, 128 * ii:128 * (ii + 1)], src[:, co:co + 128], self.ident_bf[:]),
                         reads=[tsrc, self.t_const], writes=[self.tb[bk]])
                P.op("act", lambda e, pb=pb, nci=nci, c0_=c0_, dst3=dst3: e.activation(dst3[:, c0_:c0_ + nci, :], pb[:, 0:128 * nci].rearrange("p (a b) -> p a b", a=nci), AF.Identity),
                     reads=[self.tb[bk]], writes=[tdst])
        DTs = self.scr["DT"]
        P.dma("sp", dtt[:, 0:2, :], DTs[cbase[0]:cbase[0] + CTX, :].rearrange("(i p) c -> p i c", p=128), writes=[t_dt])
        P.dma("sp", dtt[:, 2:18, :], DTs[cbase[1]:cbase[1] + SEQ, :].rearrange("(i p) c -> p i c", p=128), writes=[t_dt])
        P.op("dve", lambda e: e.tensor_tensor(dtt, dtt, dtb.unsqueeze(1).to_broadcast([128, 18, 32]), ALU.add), reads=[t_dt, t_c], writes=[t_dt])
        P.op("act", lambda e: e.activation(dtt, dtt, AF.Exp), reads=[t_dt], writes=[t_dt])
        P.op("act", lambda e: e.activation(dtt, dtt, AF.Ln, bias=1.0), reads=[t_dt], writes=[t_dt])
        P.op("act", lambda e: e.activation(abc, abc, AF.Exp), reads=[t_c], writes=[t_c])
        P.op("dve", lambda e: e.scalar_tensor_tensor(dta, dtt, -1.0, abc.unsqueeze(1).to_broadcast([128, 18, 32]), ALU.mult, ALU.mult), reads=[t_dt, t_c], writes=[t_dt])
        P.barrier()
        ar.off = mark
        E = ar.alloc([128, 16, 128], BF16)
        M = ar.alloc([128, 16, 128], BF16)
        cbT = ar.alloc([128, 4, 128], BF16)
        xdt = ar.alloc([128, 1024], BF16)
        xdtw = ar.alloc([128, 1024], BF16)
        xsD = ar.alloc([128, 1024], BF16)
        st = ar.alloc([128, 1024], F32)
        Sbf = ar.alloc([128, 1024], BF16)
        tt_ = ar.alloc([128, 1024], F32)
        yy = ar.alloc([128, 1024], F32)
        yf = ar.alloc([128, 1024], F32)
        zt = ar.alloc([128, 1024], F32)
        sq = ar.alloc([128, 1024], F32)
        ynb = ar.alloc([128, 1024], BF16)
        ysT = ar.alloc([128, 8, 128], BF16)
        negA = ar.alloc([128, 16], F32)
        eA = ar.alloc([128, 16], F32)
        ATs = ar.alloc([16, 128], F32)
        cdec = ar.alloc([128, 16], F32)
        ss = ar.alloc([128, 4], F32)
        t_E, t_M, t_cb, t_xdt, t_xdtw, t_xsD, t_st, t_Sbf, t_tt, t_yy, t_yf, t_z, t_sq, t_ynb, t_ysT = [T() for _ in range(15)]
        t_A, t_cd, t_ss = T(), T(), T()
        t_yfd = [T() for _ in range(18)]
        YF = self.scr["YF"]
        for dr in range(2):
            order = list(range(18)) if dr == 0 else [1, 0] + list(range(17, 1, -1))
            last = 127 if dr == 0 else 0
            for oi, ci in enumerate(order):
                first = oi == 0
                final = oi == 17
                gt = tiles[ci]
                skip_out = (not need_ctx) and ci < 2
                cs = 128 * ci
                dd = dta[:, ci, 16 * dr:16 * dr + 16]
                bkA = self.rbank()
                P.op("pe", lambda e, bkA=bkA, dd=dd, dr=dr: e.matmul(self.ps[:, bkA, 0:16], tri[:, dr, :], dd, start=True, stop=True), reads=[t_c, t_dt], writes=[self.tb[bkA]])
                P.op("pe", lambda e, bkA=bkA, dd=dd, dr=dr: e.matmul(self.ps[0:16, bkA, 128:256], dd, tri[:, dr, :], start=True, stop=True), reads=[t_c, t_dt], writes=[self.tb[bkA]])
                P.op("dve", lambda e, bkA=bkA: e.tensor_scalar(negA, self.ps[:, bkA, 0:16], -1.0, None, ALU.mult), reads=[self.tb[bkA]], writes=[t_A])
                P.op("act", lambda e, bkA=bkA: e.activation(eA, self.ps[:, bkA, 0:16], AF.Exp), reads=[self.tb[bkA]], writes=[t_A])
                P.op("dve", lambda e, bkA=bkA: e.tensor_copy(ATs, self.ps[0:16, bkA, 128:256]), reads=[self.tb[bkA]], writes=[t_A])
                bkC = self.rbank()
                for g in range(4):
                    P.op("pe", lambda e, bkC=bkC, g=g, cs=cs: e.matmul(self.ps[:, bkC, 128 * g:128 * (g + 1)], BT[:, g, cs:cs + 128], CT[:, g, cs:cs + 128], start=True, stop=True),
                         reads=[t_BT, t_CT], writes=[self.tb[bkC]])
                P.op("act", lambda e, bkC=bkC: e.activation(cbT, self.ps[:, bkC, :].rearrange("p (a b) -> p a b", a=4), AF.Identity), reads=[self.tb[bkC]], writes=[t_cb])
                for g in range(4):
                    bkB = self.rbank()
                    for hh in range(4):
                        h = 4 * g + hh
                        P.op("pe", lambda e, bkB=bkB, hh=hh, h=h: e.matmul(self.ps[:, bkB, 128 * hh:128 * (hh + 1)], onehot[:, h, :], ATs, start=True, stop=False), reads=[t_c, t_A], writes=[self.tb[bkB]])
                        P.op("pe", lambda e, bkB=bkB, hh=hh, dr=dr: e.matmul(self.ps[:, bkB, 128 * hh:128 * (hh + 1)], self.ident_bf[:], mneg[:, dr, :], start=False, stop=True), reads=[t_c, self.t_const], writes=[self.tb[bkB]])
                    for hh in range(4):
                        h = 4 * g + hh
                        P.op("act", lambda e, bkB=bkB, hh=hh, h=h: e.activation(E[:, h, :], self.ps[:, bkB, 128 * hh:128 * (hh + 1)], AF.Exp, bias=negA[:, h:h + 1]), reads=[self.tb[bkB], t_A], writes=[t_E])
                    if not final:
                        P.op("act", lambda e, bkB=bkB, g=g, last=last: e.activation(cdec[:, 4 * g:4 * g + 4], self.ps[:, bkB, :].rearrange("p (a b) -> p a b", a=4)[:, :, last], AF.Exp), reads=[self.tb[bkB]], writes=[t_cd])
                    P.op("pool", lambda e, g=g: e.tensor_tensor(M[:, 4 * g:4 * g + 4, :], E[:, 4 * g:4 * g + 4, :], cbT[:, g:g + 1, :].to_broadcast([128, 4, 128]), ALU.mult), reads=[t_E, t_cb], writes=[t_M])
                xs3 = xs_tok[:, ci, :].rearrange("p (h d) -> p h d", h=16)
                P.op("dve", lambda e, xs3=xs3, ci=ci, dr=dr: e.tensor_tensor(xdt.rearrange("p (h d) -> p h d", h=16), xs3, dtt[:, ci, 16 * dr:16 * dr + 16].unsqueeze(2).to_broadcast([128, 16, 64]), ALU.mult),
                     reads=[t_xs, t_dt], writes=[t_xdt])
                if dr == 1 and not skip_out:
                    P.op("pool", lambda e, ci=ci: e.tensor_tensor(xsD, xs_tok[:, ci, :], dxbc, ALU.mult), reads=[t_xs, t_c], writes=[t_xsD])
                if not skip_out:
                    for h in range(16):
                        o = self.ps[:, h // 8, 64 * (h % 8):64 * (h % 8 + 1)]
                        if dr == 1:
                            P.op("pe", lambda e, o=o, h=h: e.matmul(o, self.ident_bf[:], xsD[:, 64 * h:64 * (h + 1)], start=True, stop=False), reads=[t_xsD, self.t_const], writes=[self.tb[h // 8]])
                        P.op("pe", lambda e, o=o, h=h, dr=dr: e.matmul(o, M[:, h, :], xdt[:, 64 * h:64 * (h + 1)], start=(dr == 0), stop=True), reads=[t_M, t_xdt], writes=[self.tb[h // 8]])
                    if not first:
                        for g in range(4):
                            P.op("pe", lambda e, g=g, cs=cs: e.matmul(self.ps[:, 2 + g // 2, 256 * (g % 2):256 * (g % 2 + 1)], CT[:, g, cs:cs + 128], Sbf[:, 256 * g:256 * (g + 1)], start=True, stop=True),
                                 reads=[t_CT, t_Sbf], writes=[self.tb[2 + g // 2]])
                        P.op("dve", lambda e: e.tensor_tensor(tt_.rearrange("p (h d) -> p h d", h=16), self.ps[:, 2:4, :].rearrange("p a (h d) -> p (a h) d", d=64), eA.unsqueeze(2).to_broadcast([128, 16, 64]), ALU.mult),
                             reads=[self.tb[2], self.tb[3], t_A], writes=[t_tt])
                        P.op("dve", lambda e: e.tensor_tensor(yy.rearrange("p (a c) -> p a c", a=2), self.ps[:, 0:2, :], tt_.rearrange("p (a c) -> p a c", a=2), ALU.add),
                             reads=[self.tb[0], self.tb[1], t_tt], writes=[t_yy])
                    else:
                        P.op("dve", lambda e: e.tensor_copy(yy.rearrange("p (a c) -> p a c", a=2), self.ps[:, 0:2, :]), reads=[self.tb[0], self.tb[1]], writes=[t_yy])
                    if dr == 0:
                        P.dma("sp", YF[128 * gt:128 * (gt + 1), :], yy, reads=[t_yy], writes=[t_yfd[ci]])
                    else:
                        P.dma("sp", yf, YF[128 * gt:128 * (gt + 1), :], reads=[t_yfd[ci]], writes=[t_yf])
                        P.dma("sp", zt, self.scr["Z"][128 * gt:128 * (gt + 1), :], writes=[t_z])
                        P.op("pool", lambda e: e.tensor_tensor(yy, yy, yf, ALU.add), reads=[t_yy, t_yf], writes=[t_yy])
                        P.op("act", lambda e: e.activation(zt, zt, AF.Silu), reads=[t_z], writes=[t_z])
                        P.op("pool", lambda e: e.tensor_tensor(yy, yy, zt, ALU.mult), reads=[t_yy, t_z], writes=[t_yy])
                        P.op("pool", lambda e: e.tensor_tensor(sq, yy, yy, ALU.mult), reads=[t_yy], writes=[t_sq])
                        P.op("dve", lambda e: e.reduce_sum(ss, sq.rearrange("p (g c) -> p g c", g=4), mybir.AxisListType.X), reads=[t_sq], writes=[t_ss])
                        P.op("act", lambda e: e.activation(ss, ss, AF.Sqrt, bias=1e-5, scale=1.0 / 256), reads=[t_ss], writes=[t_ss])
                        P.op("dve", lambda e: e.reciprocal(ss, ss), reads=[t_ss], writes=[t_ss])
                        P.op("dve", lambda e: e.tensor_tensor(sq.rearrange("p (g c) -> p g c", g=4), yy.rearrange("p (g c) -> p g c", g=4), ss.unsqueeze(2).to_broadcast([128, 4, 256]), ALU.mult),
                             reads=[t_yy, t_ss], writes=[t_sq])
                        P.op("pool", lambda e: e.tensor_tensor(ynb, sq, nwbc, ALU.mult), reads=[t_sq, t_c], writes=[t_ynb])
                        bk = self.rbank()
                        pb = self.psb(bk)
                        for kc in range(8):
                            P.op("pe", lambda e, pb=pb, kc=kc: e.transpose(pb[:, 128 * kc:128 * (kc + 1)], ynb[:, 128 * kc:128 * (kc + 1)], self.ident_bf[:]), reads=[t_ynb, self.t_const], writes=[self.tb[bk]])
                        P.op("act", lambda e, pb=pb: e.activation(ysT, pb.rearrange("p (a b) -> p a b", a=8), AF.Identity), reads=[self.tb[bk]], writes=[t_ysT])
                        P.dma("sp", self.scr["YST"][:, 128 * gt:128 * (gt + 1)].rearrange("(kc p) c -> p kc c", p=128), ysT, reads=[t_ysT])
                if not final:
                    P.op("pool", lambda e, last=last: e.tensor_tensor(xdtw.rearrange("p (h d) -> p h d", h=16), xdt.rearrange("p (h d) -> p h d", h=16), E[:, :, last:last + 1].to_broadcast([128, 16, 64]), ALU.mult),
                         reads=[t_xdt, t_E], writes=[t_xdtw])
                    for g in range(4):
                        P.op("pe", lambda e, g=g, ci=ci: e.matmul(self.ps[:, 2 + g // 2, 256 * (g % 2):256 * (g % 2 + 1)], B_tok[:, ci, 128 * g:128 * (g + 1)], xdtw[:, 256 * g:256 * (g + 1)], start=True, stop=True),
                             reads=[t_Bt, t_xdtw], writes=[self.tb[2 + g // 2]])
                    if first:
                        P.op("dve", lambda e: e.tensor_copy(st.rearrange("p (a c) -> p a c", a=2), self.ps[:, 2:4, :]), reads=[self.tb[2], self.tb[3]], writes=[t_st])
                    else:
                        P.op("pool", lambda e: e.tensor_tensor(st.rearrange("p (h d) -> p h d", h=16), st.rearrange("p (h d) -> p h d", h=16), cdec.unsqueeze(2).to_broadcast([128, 16, 64]), ALU.mult),
                             reads=[t_st, t_cd], writes=[t_st])
                        P.op("dve", lambda e: e.tensor_tensor(st.rearrange("p (a c) -> p a c", a=2), st.rearrange("p (a c) -> p a c", a=2), self.ps[:, 2:4, :], ALU.add), reads=[t_st, self.tb[2], self.tb[3]], writes=[t_st])
                    P.op("act", lambda e: e.activation(Sbf, st, AF.Identity), reads=[t_st], writes=[t_Sbf])
        P.barrier()

    def phase_na(self, l, b, need_ctx):
        P, ar = self.P, self.ar
        ar.reset()
        cbase = [CTX * b, 2 * CTX + SEQ * b]
        QT = ar.alloc([128, 8, TS], BF16)
        KT = ar.alloc([128, 8, TS], BF16)
        V = ar.alloc([128, 18, 1024], BF16)
        t_q = T()
        for (dst, nm) in ((QT, "QT"), (KT, "KT")):
            src = self.scr[nm].rearrange("(a p) c -> p a c", p=128)
            P.dma("sp", dst[:, :, 0:CTX], src[:, :, cbase[0]:cbase[0] + CTX], writes=[t_q])
            P.dma("sp", dst[:, :, CTX:TS], src[:, :, cbase[1]:cbase[1] + SEQ], writes=[t_q])
        Vs = self.scr["V"]
        P.dma("sp", V[:, 0:2, :], Vs[cbase[0]:cbase[0] + CTX, :].rearrange("(i p) c -> p i c", p=128), writes=[t_q])
        P.dma("sp", V[:, 2:18, :], Vs[cbase[1]:cbase[1] + SEQ, :].rearrange("(i p) c -> p i c", p=128), writes=[t_q])
        ones = ar.alloc([128, 64], BF16)
        P.op("pool", lambda e: e.memset(ones, 1.0), writes=[t_q])
        rp = [ar.alloc([128, 960], BF16) for _ in range(2)]
        t_rp = [T(), T()]
        PTi = [ar.alloc([128, 896], BF16) for _ in range(2)]
        PTe = [ar.alloc([128, 768], BF16) for _ in range(2)]
        t_pi, t_pe = [T(), T()], [T(), T()]
        for i in range(2):
            P.op("pool", lambda e, i=i: e.memset(PTi[i], 0.0), writes=[t_pi[i]])
            P.op("pool", lambda e, i=i: e.memset(rp[i], 0.0), writes=[t_rp[i]])
        oT = [ar.alloc([64, TS], BF16) for _ in range(2)]
        t_o = [T(), T()]
        rec = [ar.alloc([64, 128], F32) for _ in range(2)]
        t_rec = [T(), T()]
        it = 0
        for h in range(16):
            pb = 64 * (h % 2)
            a_ = h // 2
            rp_, trp = rp[h % 2], t_rp[h % 2]
            P.dma("pool", rp_[0:64, :], self.inp["rpbT"][l, h], writes=[trp])
            P.dma("pool", rp_[64:128, 64:960], self.inp["rpbT"][l, h][:, 0:896], writes=[trp])
            o_, to = oT[h % 2], t_o[h % 2]
            Qh = QT[pb:pb + 64, a_, :]
            Kh = KT[pb:pb + 64, a_, :]
            qlist = [("lat", I) for I in range(16)] + ([("ctx", 0), ("ctx", 1)] if need_ctx else [])
            for (kind, I) in qlist:
                par = it % 2
                it += 1
                bkL = 2 * par
                bkM = 4 + par
                psL = self.ps[:, bkL:bkL + 2, :].rearrange("p a c -> p (a c)")
                tL = [self.tb[bkL], self.tb[bkL + 1]]
                psM = self.ps[:, bkM, :]
                tM = self.tb[bkM]
                rec_, trec = rec[par], t_rec[par]
                if kind == "ctx":
                    qc = 128 * I
                    PT, tP = PTe[par], t_pe[par]
                    for c in range(2):
                        P.op("pe", lambda e, c=c, qc=qc, Kh=Kh, Qh=Qh, psM=psM: e.matmul(psM[:, 128 * c:128 * (c + 1)], Kh[:, 128 * c:128 * (c + 1)], Qh[:, qc:qc + 128], start=True, stop=True),
                             reads=[t_q], writes=[tM])
                    P.op("act", lambda e, PT=PT, psM=psM: e.activation(PT[:, 512:768], psM[:, 0:256], AF.Exp), reads=[tM], writes=[tP])
                    pv = [(c, 512 + 128 * c) for c in range(2)]
                else:
                    qc = CTX + 128 * I
                    interior = 2 <= I <= 13
                    PT, tP = (PTi[par], t_pi[par]) if interior else (PTe[par], t_pe[par])
                    lc = 640 if interior else 512
                    if interior:
                        J0 = I - 2
                        nsl = 5
                    else:
                        J0 = 0 if I < 2 else 12
                        nsl = 4
                    cbk = {s_: s_ for s_ in range(nsl)}
                    for s in range(nsl):
                        J = J0 + s
                        drr0 = 7 - 2 * J + 2 * I
                        kc0 = CTX + 128 * J
                        oc = 128 * s
                        o = psL[:, oc:oc + 128]
                        tbk = tL[s // 4]
                        P.op("pe", lambda e, o=o, kc0=kc0, qc=qc, Kh=Kh, Qh=Qh: e.matmul(o, Kh[:, kc0:kc0 + 128], Qh[:, qc:qc + 128], start=True, stop=False),
                             reads=[t_q], writes=[tbk])
                        P.op("pe", lambda e, o=o, drr0=drr0, rp_=rp_: e.matmul(o, self.ident_bf[:], rp_[:, 64 * drr0:64 * drr0 + 128], start=False, stop=True),
                             reads=[trp, self.t_const], writes=[tbk])
                    for c in range(2):
                        P.op("pe", lambda e, c=c, qc=qc, Kh=Kh, Qh=Qh, psM=psM: e.matmul(psM[:, 128 * c:128 * (c + 1)], Kh[:, 128 * c:128 * (c + 1)], Qh[:, qc:qc + 128], start=True, stop=True),
                             reads=[t_q], writes=[tM])
                    P.op("act", lambda e, PT=PT, psL=psL, lc=lc: e.activation(PT[:, 0:lc], psL[:, 0:lc], AF.Exp), reads=tL, writes=[tP])
                    if interior:
                        P.op("pool", lambda e, PT=PT: e.memset(PT[0:64, 64:128], 0.0), reads=[], writes=[tP])
                        P.op("pool", lambda e, PT=PT: e.memset(PT[0:64, 512:576], 0.0), reads=[], writes=[tP])
                        P.op("pool", lambda e, PT=PT: e.memset(PT[64:128, 512:640], 0.0), reads=[], writes=[tP])
                    P.op("act", lambda e, PT=PT, psM=psM, lc=lc: e.activation(PT[:, lc:lc + 256], psM[:, 0:256], AF.Exp), reads=[tM], writes=[tP])
                    pv = [(2 + J0 + s, 128 * cbk[s]) for s in range(nsl)] + [(c, lc + 128 * c) for c in range(2)]
                npv = len(pv)
                for i_, (vt, pc) in enumerate(pv):
                    P.op("pe", lambda e, vt=vt, pc=pc, i_=i_, npv=npv, PT=PT, psM=psM, h=h: e.matmul(psM[0:64, 256:384], V[:, vt, 64 * h:64 * (h + 1)], PT[:, pc:pc + 128], start=(i_ == 0), stop=(i_ == npv - 1)),
                         reads=[t_q, tP], writes=[tM])
                for i_, (vt, pc) in enumerate(pv):
                    P.op("pe", lambda e, pc=pc, i_=i_, npv=npv, PT=PT, psM=psM: e.matmul(psM[0:64, 384:512], ones, PT[:, pc:pc + 128], start=(i_ == 0), stop=(i_ == npv - 1)),
                         reads=[t_q, tP], writes=[tM])
                P.op("dve", lambda e, rec_=rec_, psM=psM: e.reciprocal(rec_, psM[0:64, 384:512]), reads=[tM], writes=[trec])
                oc_ = qc
                P.op("dve", lambda e, rec_=rec_, psM=psM, o_=o_, oc_=oc_: e.tensor_tensor(o_[:, oc_:oc_ + 128], psM[0:64, 256:384], rec_, ALU.mult), reads=[tM, trec], writes=[to])
            ON = self.scr["ONT"]
            if need_ctx:
                P.dma("sp", ON[64 * h:64 * (h + 1), cbase[0]:cbase[0] + CTX], o_[:, 0:CTX], reads=[to])
            P.dma("sp", ON[64 * h:64 * (h + 1), cbase[1]:cbase[1] + SEQ], o_[:, CTX:TS], reads=[to])
        P.barrier()

    def phase_pool(self, l, b, need_ctx):
        P, ar = self.P, self.ar
        ar.reset()
        cbase = [CTX * b, 2 * CTX + SEQ * b]
        PW = 8 + CTX + 16 + SEQ + 8
        o_c, o_l = 8, 8 + CTX + 16
        u = [ar.alloc([128, PW], F32) for _ in range(2)]
        A = ar.alloc([128, PW], F32)
        B = ar.alloc([128, PW], F32)
        inv = ar.alloc([128, PW], F32)
        pl = [ar.alloc([128, TS], BF16) for _ in range(2)]
        pw = ar.alloc([128, 2, 192], BF16)
        psc = ar.alloc([128, 8], F32)
        stg = [ar.alloc([128, TS], BF16) for _ in range(2)]
        t_u, t_A, t_B, t_inv, t_pl, t_pw, t_psc, t_stg = [T(), T()], T(), T(), T(), [T(), T()], T(), T(), [T(), T()]
        for i in range(2):
            P.op("pool", lambda e, i=i: e.memset(u[i], 0.0), writes=[t_u[i]])
        P.dma("sp", psc, self.inp["pool_scaleT"][l], writes=[t_psc])
        PL = self.scr["PLT"]
        PM = self.scr["PMT"]
        si = 0
        for g in range(4):
            nlev = g + 1
            P.dma("sp", inv, self.inp["invcnt"][g].partition_broadcast(128), writes=[t_inv])
            P.dma("pool", pw[:, 0, :], self.inp["pool_w"][l, g, 0:128, :], writes=[t_pw])
            P.dma("pool", pw[0:64, 1, :], self.inp["pool_w"][l, g, 128:192, :], writes=[t_pw])
            for kt in range(2):
                n = 128 if kt == 0 else 64
                r0 = 192 * g + 128 * kt
                eng = "pool" if kt == 0 else "dve"
                u_, tu = u[kt], t_u[kt]
                P.dma("sp", u_[0:n, o_c:o_c + CTX], PL[r0:r0 + n, cbase[0]:cbase[0] + CTX], writes=[tu])
                P.dma("sp", u_[0:n, o_l:o_l + SEQ], PL[r0:r0 + n, cbase[1]:cbase[1] + SEQ], writes=[tu])
                srcb, tsrc = u_, tu
                sh = [(1, 0), (1, 1), (2, 2), (4, 4)]
                lo, hi = 0, PW
                for lv in range(nlev):
                    dstb, tdst = (A, t_A) if lv % 2 == 0 else (B, t_B)
                    s1, s2 = sh[lv]
                    lo2, hi2 = lo + s1, hi - s2
                    P.op(eng, lambda e, dstb=dstb, srcb=srcb, lo2=lo2, hi2=hi2, s1=s1, s2=s2, n=n: e.tensor_tensor(dstb[0:n, lo2:hi2], srcb[0:n, lo2 - s1:hi2 - s1], srcb[0:n, lo2 + s2:hi2 + s2], ALU.add),
                         reads=[tsrc], writes=[tdst])
                    srcb, tsrc, lo, hi = dstb, tdst, lo2, hi2
                oth, toth = (B, t_B) if srcb is A else (A, t_A)
                P.op(eng, lambda e, oth=oth, srcb=srcb, n=n: e.tensor_tensor(oth[0:n, 8:PW - 8], srcb[0:n, 8:PW - 8], inv[0:n, 8:PW - 8], ALU.mult), reads=[tsrc, t_inv], writes=[toth])
                P.op(eng, lambda e, oth=oth, u_=u_, kt=kt, n=n: e.tensor_tensor(pl[kt][0:n, 0:CTX], oth[0:n, o_c:o_c + CTX], u_[0:n, o_c:o_c + CTX], ALU.subtract), reads=[toth, tu], writes=[t_pl[kt]])
                P.op(eng, lambda e, oth=oth, u_=u_, kt=kt, n=n: e.tensor_tensor(pl[kt][0:n, CTX:TS], oth[0:n, o_l:o_l + SEQ], u_[0:n, o_l:o_l + SEQ], ALU.subtract), reads=[toth, tu], writes=[t_pl[kt]])
            for mt in range(2):
                mw = 128 if mt == 0 else 64
                sg, tsg = stg[si % 2], t_stg[si % 2]
                si += 1
                chunks = ([(0, CTX)] if need_ctx else []) + [(CTX + 512 * q, 512) for q in range(4)]
                for (c0, nc_) in chunks:
                    bk = self.bank()
                    for kt in range(2):
                        kw = 128 if kt == 0 else 64
                        P.op("pe", lambda e, bk=bk, kt=kt, kw=kw, mt=mt, mw=mw, c0=c0, nc_=nc_: e.matmul(self.ps[0:mw, bk, 0:nc_], pw[0:kw, kt, 128 * mt:128 * mt + mw], pl[kt][0:kw, c0:c0 + nc_], start=(kt == 0), stop=(kt == 1)),
                             reads=[t_pw, t_pl[kt]], writes=[self.tb[bk]])
                    P.op("act", lambda e, bk=bk, sg=sg, mw=mw, c0=c0, nc_=nc_, g=g, mt=mt: e.activation(sg[0:mw, c0:c0 + nc_], self.ps[0:mw, bk, 0:nc_], AF.Identity, scale=psc[0:mw, 2 * g + mt:2 * g + mt + 1]),
                         reads=[self.tb[bk], t_psc], writes=[tsg])
                r0 = 192 * g + 128 * mt
                if need_ctx:
                    P.dma("sp", PM[r0:r0 + mw, cbase[0]:cbase[0] + CTX], sg[0:mw, 0:CTX], reads=[tsg])
                P.dma("sp", PM[r0:r0 + mw, cbase[1]:cbase[1] + SEQ], sg[0:mw, CTX:TS], reads=[tsg])
        P.barrier()

    def phase_fnet(self, l, b, need_ctx):
        P, ar = self.P, self.ar
        ar.reset()
        cbase = [CTX * b, 2 * CTX + SEQ * b]
        ut = ar.alloc([128, 18, 768], BF16)
        A1 = ar.alloc([128, 8, TS], BF16)
        A2 = ar.alloc([128, 8, TS], BF16)
        cd = ar.alloc([128, 2, 192], BF16)
        sd = ar.alloc([128, 2, 192], BF16)
        cq = [ar.alloc([128, 16, 512], BF16) for _ in range(2)]
        sq_ = [ar.alloc([128, 16, 512], BF16) for _ in range(2)]
        t_u, t_A1, t_A2, t_cd, t_cq = T(), T(), T(), T(), [T(), T()]
        FN = self.scr["FN"]
        if need_ctx:
            P.dma("sp", ut[:, 0:2, :], FN[cbase[0]:cbase[0] + CTX, :].rearrange("(i p) c -> p i c", p=128), writes=[t_u])
        P.dma("sp", ut[:, 2:18, :], FN[cbase[1]:cbase[1] + SEQ, :].rearrange("(i p) c -> p i c", p=128), writes=[t_u])
        for (dst, nm) in ((cd, "dftcd"), (sd, "dftsdn")):
            P.dma("sp", dst[:, 0, :], self.inp[nm][0:128, :], writes=[t_cd])
            P.dma("sp", dst[0:64, 1, :], self.inp[nm][128:192, :], writes=[t_cd])
        ctiles = [(192 * hd + 128 * kt, 128 if kt == 0 else 64) for hd in range(4) for kt in range(2)]
        jobs = [(CTX + 512 * q, 512, 16, 2, "2048", 512 * q) for q in range(4)]
        if need_ctx:
            jobs.append((0, 256, 2, 0, "256", 0))
        for ji, (c0, nc_, na, a0, nm, sc0) in enumerate(jobs):
            c_, s_, tcq = cq[ji % 2], sq_[ji % 2], t_cq[ji % 2]
            P.dma("sp", c_[:, 0:na, 0:nc_], self.inp["dftc" + nm][:, sc0:sc0 + nc_].rearrange("(a p) c -> p a c", p=128), writes=[tcq])
            P.dma("sp", s_[:, 0:na, 0:nc_], self.inp["dfts" + nm][:, sc0:sc0 + nc_].rearrange("(a p) c -> p a c", p=128), writes=[tcq])
            for ti, (ch0, w) in enumerate(ctiles):
                for (mat, dstA, tA) in ((c_, A1, t_A1), (s_, A2, t_A2)):
                    bk = self.bank()
                    for a in range(na):
                        P.op("pe", lambda e, bk=bk, a=a, a0=a0, na=na, ch0=ch0, w=w, mat=mat, nc_=nc_: e.matmul(self.ps[0:w, bk, 0:nc_], ut[:, a0 + a, ch0:ch0 + w], mat[:, a, 0:nc_], start=(a == 0), stop=(a == na - 1)),
                             reads=[t_u, tcq], writes=[self.tb[bk]])
                    if mat is c_:
                        P.op("act", lambda e, bk=bk, dstA=dstA, ti=ti, w=w, c0=c0, nc_=nc_: e.activation(dstA[0:w, ti, c0:c0 + nc_], self.ps[0:w, bk, 0:nc_], AF.Identity), reads=[self.tb[bk]], writes=[tA])
                    else:
                        P.op("dve", lambda e, bk=bk, dstA=dstA, ti=ti, w=w, c0=c0, nc_=nc_: e.tensor_copy(dstA[0:w, ti, c0:c0 + nc_], self.ps[0:w, bk, 0:nc_]), reads=[self.tb[bk]], writes=[tA])
        stg = [ar.alloc([128, TS], BF16) for _ in range(2)]
        t_stg = [T(), T()]
        FM = self.scr["FMT"]
        si = 0
        chunks = ([(0, CTX)] if need_ctx else []) + [(CTX + 512 * q, 512) for q in range(4)]
        for hd in range(4):
            for mt in range(2):
                mw = 128 if mt == 0 else 64
                sg, tsg = stg[si % 2], t_stg[si % 2]
                si += 1
                for (c0, nc_) in chunks:
                    bk = self.bank()
                    k = 0
                    for (mat, srcA, tA) in ((cd, A1, t_A1), (sd, A2, t_A2)):
                        for kt in range(2):
                            kw = 128 if kt == 0 else 64
                            P.op("pe", lambda e, bk=bk, k=k, kt=kt, kw=kw, mt=mt, mw=mw, mat=mat, srcA=srcA, hd=hd, c0=c0, nc_=nc_: e.matmul(self.ps[0:mw, bk, 0:nc_], mat[0:kw, kt, 128 * mt:128 * mt + mw], srcA[0:kw, 2 * hd + kt, c0:c0 + nc_], start=(k == 0), stop=(k == 3)),
                                 reads=[t_cd, tA], writes=[self.tb[bk]])
                            k += 1
                    P.op("act", lambda e, bk=bk, sg=sg, mw=mw, c0=c0, nc_=nc_: e.activation(sg[0:mw, c0:c0 + nc_], self.ps[0:mw, bk, 0:nc_], AF.Identity), reads=[self.tb[bk]], writes=[tsg])
                r0 = 192 * hd + 128 * mt
                if need_ctx:
                    P.dma("sp", FM[r0:r0 + mw, cbase[0]:cbase[0] + CTX], sg[0:mw, 0:CTX], reads=[tsg])
                P.dma("sp", FM[r0:r0 + mw, cbase[1]:cbase[1] + SEQ], sg[0:mw, CTX:TS], reads=[tsg])
        P.barrier()

    def resid_ln(self, l, ntile, tiles, psview, xt, t_x, gi, gam, bet, t_w, bufs, addb, eng2="pool"):
        P = self.P
        mv, st, rstd, tmp, t_mv, t_tmp = bufs
        for i in range(ntile):
            row = tile_modrow(tiles[i])
            gb = self.gbc[:, row, :].rearrange("p (a c) -> p a c", a=2)
            tv = tmp[:, i, :].rearrange("p (a c) -> p a c", a=2)
            if addb is not None:
                P.op("dve", lambda e, i=i, tv=tv: e.tensor_tensor(tv, psview(i), addb.rearrange("p (a c) -> p a c", a=2), ALU.add), reads=self.ps_reads(i) + [t_w], writes=[t_tmp[i]])
                P.op(eng2, lambda e, i=i, row=row: e.tensor_tensor(tmp[:, i, :], tmp[:, i, :], self.gbc[:, row, :], ALU.mult), reads=[t_tmp[i], self.t_mod], writes=[t_tmp[i]])
            else:
                P.op("dve", lambda e, i=i, tv=tv, gb=gb: e.tensor_tensor(tv, psview(i), gb, ALU.mult), reads=self.ps_reads(i) + [self.t_mod], writes=[t_tmp[i]])
            P.op("dve", lambda e, i=i: e.scalar_tensor_tensor(xt[:, i, :], xt[:, i, :], ALPHA, tmp[:, i, :], ALU.mult, ALU.add), reads=[t_x, t_tmp[i]], writes=[t_x])
        self.ln_stats(xt, t_x, ntile, mv, t_mv, st, rstd, 1e-6)
        for i in range(ntile):
            P.op("dve", lambda e, i=i: e.tensor_scalar(xt[:, i, :], xt[:, i, :], mv[:, i, 0:1], rstd[:, i:i + 1], ALU.subtract, ALU.mult), reads=[t_x, t_mv], writes=[t_x])
            P.op(eng2, lambda e, i=i: e.tensor_tensor(xt[:, i, :], xt[:, i, :], gam, ALU.mult), reads=[t_x, t_w], writes=[t_x])
            P.op(eng2, lambda e, i=i: e.tensor_tensor(xt[:, i, :], xt[:, i, :], bet, ALU.add), reads=[t_x, t_w], writes=[t_x])

    def phase_merge(self, l, need_ctx):
        P, ar = self.P, self.ar
        ar.reset()
        Wssd = ar.alloc([128, 8, 1024], BF16)
        Wna = ar.alloc([128, 8, 1024], BF16)
        Wpl = ar.alloc([128, 8, 1024], BF16)
        Wfn = ar.alloc([128, 8, 1024], BF16)
        Wmx = ar.alloc([128, 8, 1024], BF16)
        gam = ar.alloc([128, 1024], F32)
        bet = ar.alloc([128, 1024], F32)
        t_w = T()
        t_ws, t_wn, t_wm, t_wp, t_wf = T(), T(), T(), T(), T()
        P.dma("pool", Wssd, self.inp["ssd_out_w"][l].rearrange("(a p) c -> p a c", p=128), writes=[t_ws])
        P.dma("pool", Wna, self.inp["na_out_w"][l].rearrange("(a p) c -> p a c", p=128), writes=[t_wn])
        for (W_, nm, tw_) in ((Wpl, "pool_out_w", t_wp), (Wfn, "fnet_out_w", t_wf)):
            for g in range(4):
                P.dma("pool", W_[:, 2 * g, :], self.inp[nm][l, 192 * g:192 * g + 128, :], writes=[tw_])
                P.dma("pool", W_[0:64, 2 * g + 1, :], self.inp[nm][l, 192 * g + 128:192 * g + 192, :], writes=[tw_])
        P.dma("pool", Wmx, self.inp["mix_out_w"][l].rearrange("(a p) c -> p a c", p=128), writes=[t_wm])
        t_wof = {"YST": t_ws, "ONT": t_wn, "PMT": t_wp, "FMT": t_wf}
        P.dma("sp", gam, self.inp["ln1_g"][l].partition_broadcast(128), writes=[t_w])
        P.dma("sp", bet, self.inp["ln1_b"][l].partition_broadcast(128), writes=[t_w])
        br = ar.alloc([128, 8, 512], BF16)
        gt = ar.alloc([128, 8, 512], BF16)
        mg = ar.alloc([128, 8, 512], F32)
        tm = ar.alloc([128, 512], F32)
        mgb = ar.alloc([128, 8, 512], BF16)
        xt = ar.alloc([128, 4, 1024], F32)
        tmp = ar.alloc([128, 4, 1024], F32)
        mv = ar.alloc([128, 4, 2], F32)
        st = ar.alloc([128, 4, 2, 6], F32)
        rstd = ar.alloc([128, 4], F32)
        t_br, t_gt, t_mg, t_tm, t_mgb, t_x = T(), T(), T(), T(), T(), T()
        bufs = (mv, st, rstd, tmp, T(), [T() for _ in range(4)])
        ktl = [(192 * g + 128 * kt, 128 if kt == 0 else 64) for g in range(4) for kt in range(2)]
        specs = [("YST", Wssd, [(128 * a, 128) for a in range(8)]), ("ONT", Wna, [(128 * a, 128) for a in range(8)]),
                 ("PMT", Wpl, ktl), ("FMT", Wfn, ktl)]
        for ch in range(0 if need_ctx else 1, NCH):
            cs = 512 * ch
            for i in range(4):
                P.dma("sp", xt[:, i, :], self.xsrc(l, 4 * ch + i), writes=[t_x])
            for bi, (nm, W_, kts) in enumerate(specs):
                src = self.scr[nm]
                if nm in ("YST", "ONT"):
                    P.dma("sp", br[:, 0:8, :], src[:, cs:cs + 512].rearrange("(a p) c -> p a c", p=128), writes=[t_br])
                else:
                    for ti, (r0, n) in enumerate(kts):
                        P.dma("sp", br[0:n, ti, :], src[r0:r0 + n, cs:cs + 512], writes=[t_br])
                P.dma("sp", gt, self.scr["GT"][1024 * bi:1024 * (bi + 1), cs:cs + 512].rearrange("(a p) c -> p a c", p=128), writes=[t_gt])
                nk = len(kts)
                for dt_ in range(8):
                    bk = self.bank()
                    for ti, (r0, n) in enumerate(kts):
                        P.op("pe", lambda e, bk=bk, ti=ti, n=n, nk=nk, W_=W_, dt_=dt_: e.matmul(self.ps[:, bk, :], W_[0:n, ti, 128 * dt_:128 * (dt_ + 1)], br[0:n, ti, :], start=(ti == 0), stop=(ti == nk - 1)),
                             reads=[t_wof[nm], t_br], writes=[self.tb[bk]])
                    if bi == 0:
                        P.op("dve", lambda e, bk=bk, dt_=dt_: e.tensor_tensor(mg[:, dt_, :], self.ps[:, bk, :], gt[:, dt_, :], ALU.mult), reads=[self.tb[bk], t_gt], writes=[t_mg])
                    else:
                        P.op("dve", lambda e, bk=bk, dt_=dt_: e.tensor_tensor(tm, self.ps[:, bk, :], gt[:, dt_, :], ALU.mult), reads=[self.tb[bk], t_gt], writes=[t_tm])
                        P.op("pool", lambda e, dt_=dt_: e.tensor_tensor(mg[:, dt_, :], mg[:, dt_, :], tm, ALU.add), reads=[t_tm, t_mg], writes=[t_mg])
            P.op("act", lambda e: e.activation(mgb, mg, AF.Identity), reads=[t_mg], writes=[t_mgb])
            pbk = []
            for i in range(4):
                b0 = 2 * (i % 4)
                pbk.append(b0)
                for half in range(2):
                    for kc in range(8):
                        P.op("pe", lambda e, b0=b0, half=half, kc=kc, i=i: e.matmul(self.ps[:, b0 + half, :], mgb[:, kc, 128 * i:128 * (i + 1)], Wmx[:, kc, 512 * half:512 * (half + 1)], start=(kc == 0), stop=(kc == 7)),
                             reads=[t_mgb, t_wm], writes=[self.tb[b0 + half]])
            self.ps_reads = lambda i: [self.tb[2 * i], self.tb[2 * i + 1]]
            self.resid_ln(l, 4, [4 * ch + i for i in range(4)], lambda i: self.ps[:, 2 * i:2 * i + 2, :], xt, t_x, 0, gam, bet, t_w, bufs, None)
            for i in range(4):
                P.dma("sp", self.scr["X"][128 * (4 * ch + i):128 * (4 * ch + i + 1), :], xt[:, i, :], reads=[t_x])
        P.barrier()

    def phase_mlp(self, l, need_ctx, final):
        P, ar = self.P, self.ar
        ar.reset()
        Wup = ar.alloc([128, 8, 4096], BF16)
        Wdn = ar.alloc([128, 32, 1024], BF16)
        ubT = ar.alloc([128, 32], F32)
        gam = ar.alloc([128, 1024], F32)
        bet = ar.alloc([128, 1024], F32)
        dnb = ar.alloc([128, 1024], F32)
        t_w = T()
        t_wu = [T() for _ in range(4)]
        t_wd = [T() for _ in range(4)]
        t_ub = T()
        for q in range(4):
            P.dma("pool", Wup[:, :, 1024 * q:1024 * (q + 1)], self.inp["mlp_up_w"][l][:, 1024 * q:1024 * (q + 1)].rearrange("(a p) c -> p a c", p=128), writes=[t_wu[q]])
        for q in range(4):
            P.dma("pool", Wdn[:, 8 * q:8 * (q + 1), :], self.inp["mlp_down_w"][l][1024 * q:1024 * (q + 1), :].rearrange("(a p) c -> p a c", p=128), writes=[t_wd[q]])
        P.dma("sp", ubT, self.inp["mlp_up_bT"][l], writes=[t_ub])
        P.dma("sp", gam, self.inp["ln2_g"][l].partition_broadcast(128), writes=[t_w])
        P.dma("sp", bet, self.inp["ln2_b"][l].partition_broadcast(128), writes=[t_w])
        P.dma("sp", dnb, self.inp["mlp_down_b"][l].partition_broadcast(128), writes=[t_w])
        NT = 2
        h2 = ar.alloc([128, 8, 128 * NT], BF16)
        a2 = ar.alloc([128, 32, 128 * NT], BF16)
        rl = [ar.alloc([128, 128 * NT], BF16) for _ in range(2)]
        xt = ar.alloc([128, NT, 1024], F32)
        tmp = ar.alloc([128, NT, 1024], F32)
        xn = ar.alloc([128, NT, 1024], BF16)
        mv = ar.alloc([128, NT, 2], F32)
        st = ar.alloc([128, NT, 2, 6], F32)
        rstd = ar.alloc([128, NT], F32)
        t_h2, t_a2, t_rl, t_x, t_xn, t_mv = T(), T(), [T(), T()], T(), [T() for _ in range(NT)], T()
        bufs = (mv, st, rstd, tmp, t_mv, [T() for _ in range(NT)])
        first_tile = 0 if need_ctx else 4
        for c2 in range(first_tile // NT, NTILE // NT):
            tiles = [NT * c2 + i for i in range(NT)]
            for i in range(NT):
                P.dma("sp", xt[:, i, :], self.scr["X"][128 * tiles[i]:128 * (tiles[i] + 1), :], writes=[t_x])
            self.ln_stats(xt, t_x, NT, mv, t_mv, st, rstd, 1e-6)
            for i in range(NT):
                row = tile_modrow(tiles[i])
                P.op("dve", lambda e, i=i: e.tensor_scalar(xn[:, i, :], xt[:, i, :], mv[:, i, 0:1], rstd[:, i:i + 1], ALU.subtract, ALU.mult), reads=[t_x, t_mv], writes=[t_xn[i]])
                bk = self.bank()
                pb = self.psb(bk)
                for kc in range(8):
                    P.op("pe", lambda e, i=i, kc=kc, pb=pb: e.transpose(pb[:, 128 * kc:128 * (kc + 1)], xn[:, i, 128 * kc:128 * (kc + 1)], self.ident_bf[:]), reads=[t_xn[i], self.t_const], writes=[self.tb[bk]])
                for kc in range(8):
                    P.op("act", lambda e, kc=kc, pb=pb, row=row, i=i: e.activation(h2[:, kc, 128 * i:128 * (i + 1)], pb[:, 128 * kc:128 * (kc + 1)], AF.Identity, bias=self.modT[:, 2, kc, row:row + 1], scale=self.modT[:, 3, kc, row:row + 1]),
                         reads=[self.tb[bk], self.t_mod], writes=[t_h2])
            for ft in range(32):
                bk = self.bank()
                for kc in range(8):
                    P.op("pe", lambda e, bk=bk, kc=kc, ft=ft: e.matmul(self.ps[:, bk, 0:128 * NT], Wup[:, kc, 128 * ft:128 * (ft + 1)], h2[:, kc, :], start=(kc == 0), stop=(kc == 7)), reads=[t_wu[ft // 8], t_h2], writes=[self.tb[bk]])
                r_, tr = rl[ft % 2], t_rl[ft % 2]
                P.op("act", lambda e, bk=bk, ft=ft, r_=r_: e.activation(r_, self.ps[:, bk, 0:128 * NT], AF.Relu, bias=ubT[:, ft:ft + 1]), reads=[self.tb[bk], t_ub], writes=[tr])
                P.op("pool", lambda e, ft=ft, r_=r_: e.tensor_tensor(a2[:, ft, :], r_, r_, ALU.mult), reads=[tr], writes=[t_a2])
            for i in range(NT):
                for half in range(2):
                    bk = 2 * i + half
                    for ft in range(32):
                        P.op("pe", lambda e, bk=bk, ft=ft, i=i, half=half: e.matmul(self.ps[:, bk, :], a2[:, ft, 128 * i:128 * (i + 1)], Wdn[:, ft, 512 * half:512 * (half + 1)], start=(ft == 0), stop=(ft == 31)),
                             reads=[t_a2, t_wd[ft // 8]], writes=[self.tb[bk]])
            self.ps_reads = lambda i: [self.tb[2 * i], self.tb[2 * i + 1]]
            self.resid_ln(l, NT, tiles, lambda i: self.ps[:, 2 * i:2 * i + 2, :], xt, t_x, 1, gam, bet, t_w, bufs, dnb)
            for i in range(NT):
                t = tiles[i]
                if final:
                    bb, r = (t - 4) // 16, (t - 4) % 16
                    P.dma("sp", self.out[bb, 128 * r:128 * (r + 1), :], xt[:, i, :], reads=[t_x])
                else:
                    P.dma("sp", self.scr["X"][128 * t:128 * (t + 1), :], xt[:, i, :], reads=[t_x])
        P.barrier()


def build(n_layers=DEPTH, dump=(), upto=None, last=DEPTH - 1):
    k = K(n_layers, dump)
    k.eps_ln = 1e-6
    P = k.P
    k.phase_mod_init()
    phases = ["mod", "lnt", "inproj", "ssd", "na", "pool", "fnet", "merge", "mlp"]
    for l in range(n_layers):
        need_ctx = l < last
        final = l == last
        def on(nm):
            return upto is None or l < n_layers - 1 or phases.index(nm) <= phases.index(upto)
        if on("mod"):
            k.phase_mod(l, (0, 1, 2, 3, 4))
        if on("lnt"):
            k.phase_lnt(l)
        if on("inproj"):
            k.phase_inproj(l)
        for b in range(NB):
            if on("ssd"):
                k.phase_ssd(l, b, need_ctx)
        for b in range(NB):
            if on("na"):
                k.phase_na(l, b, need_ctx)
        for b in range(NB):
            if on("pool"):
                k.phase_pool(l, b, need_ctx)
            if on("fnet"):
                k.phase_fnet(l, b, need_ctx)
        if on("merge"):
            k.phase_merge(l, need_ctx)
        if on("mlp"):
            k.phase_mod(l, (5,))
            k.phase_mlp(l, need_ctx, final)
        P.barrier(new_epoch=True)
    P.emit()
    P.close()
    return k


_CACHE = {}


def kernel(**inputs):
    inputs = {k: np.asarray(v) for k, v in inputs.items()}
    consts = host_consts()
    if "k" not in _CACHE:
        _CACHE["k"] = build()
    k = _CACHE["k"]
    in_maps = []
    for core in range(8):
        m = host_layout(inputs, core)
        m.update(consts)
        in_maps.append(m)
    res = run_bass_kernel_spmd(k.P.nc, in_maps, core_ids=list(range(8)))
    out = np.concatenate([np.asarray(r["out"]) for r in res.results], axis=0)
    return out.astype(np.float32)
```
